# Optimizing a Trainium2 kernel written in Bass

```python
import jax, jax.numpy as jnp
from jax import lax
import numpy as np

D_MODEL = 2048
BATCH = 4
SEQ = 2048
DEPTH = 1
DEC_BATCH = 128
DEC_SEQ = 1
PAST_LEN = 16384
PAGE_SIZE = 128

POOL_WIDTH = D_MODEL // 2
POOL_GROUPS = 4
POOL_GROUP_DIM = POOL_WIDTH // POOL_GROUPS
POOL_WINDOWS = (2, 4, 8, 16)
POOL_STATE = max(POOL_WINDOWS) - 1
CONV_WIDTH = D_MODEL // 2
CONV_K = 3
CONV_STATE = CONV_K - 1
D_FF = ((8 * D_MODEL // 3 + 255) // 256) * 256
N_MOD = 9
IN_COLS = POOL_WIDTH + 3 * CONV_WIDTH + 2 * D_MODEL
EPS = 1e-6

kernel_name = "cond_macaron_pool_shortconv_decoder_step"


def rmsnorm(x, g):
    xf = x.astype(jnp.float32)
    inv = lax.rsqrt(jnp.mean(xf * xf, axis=-1, keepdims=True) + EPS)
    return (xf * inv).astype(x.dtype) * g


def modulate(x, g, shift, scale):
    return rmsnorm(x, g) * (1.0 + scale[:, None, :]) + shift[:, None, :]


def swiglu(h, w_gate, w_up, w_down):
    return (jax.nn.silu(h @ w_gate) * (h @ w_up)) @ w_down


def multiscale_pool(buf, n_new, pos0):
    L = buf.shape[1]
    start = L - n_new
    cs = jnp.cumsum(buf.astype(jnp.float32), axis=1)
    cs = jnp.concatenate([jnp.zeros_like(cs[:, :1]), cs], axis=1)
    idx = start + jnp.arange(n_new)
    pos = pos0 + jnp.arange(n_new)
    hi = cs[:, idx + 1]
    outs = []
    for g, w in enumerate(POOL_WINDOWS):
        sl = slice(g * POOL_GROUP_DIM, (g + 1) * POOL_GROUP_DIM)
        lo = cs[:, jnp.maximum(idx + 1 - w, 0), sl]
        cnt = jnp.minimum(pos + 1, w).astype(jnp.float32)[None, :, None]
        outs.append((hi[..., sl] - lo) / cnt)
    pooled = jnp.concatenate(outs, axis=-1).astype(buf.dtype)
    return pooled - buf[:, start:]


def causal_dwconv(buf, n_new, w, b):
    y = b
    for k in range(CONV_K):
        y = y + buf[:, k:k + n_new] * w[k]
    return y


def layer(x, c, pool_prefix, conv_prefix, pos0, w_ada, b_ada, norm1, ffn1_gate, ffn1_up, ffn1_down,
          norm2, w_in, pool_grp, pool_scale, w_branch_a, conv_w, conv_b, w_branch_b, w_o,
          norm3, ffn2_gate, ffn2_up, ffn2_down):
    B, T, _ = x.shape
    mod = jax.nn.silu(c) @ w_ada + b_ada
    sh1, sc1, g1, sh2, sc2, g2, sh3, sc3, g3 = jnp.split(mod, N_MOD, axis=-1)
    h = modulate(x, norm1, sh1, sc1)
    x = x + 0.5 * g1[:, None, :] * swiglu(h, ffn1_gate, ffn1_up, ffn1_down)
    h = modulate(x, norm2, sh2, sc2)
    z = h @ w_in
    p, cb, cc, ch, ga, gb = jnp.split(z, [POOL_WIDTH, POOL_WIDTH + CONV_WIDTH, POOL_WIDTH + 2 * CONV_WIDTH,
                                         POOL_WIDTH + 3 * CONV_WIDTH, POOL_WIDTH + 3 * CONV_WIDTH + D_MODEL], axis=-1)
    pool_buf = jnp.concatenate([pool_prefix, p], axis=1)
    a = multiscale_pool(pool_buf, T, pos0)
    a = jnp.einsum('btgc,gcd->btgd', a.reshape(B, T, POOL_GROUPS, POOL_GROUP_DIM), pool_grp)
    a = a.reshape(B, T, POOL_WIDTH) * pool_scale
    ua = a @ w_branch_a
    conv_buf = jnp.concatenate([conv_prefix, cc * ch], axis=1)
    v = cb * causal_dwconv(conv_buf, T, conv_w, conv_b)
    ub = v @ w_branch_b
    m = jax.nn.sigmoid(ga) * ua + jax.nn.sigmoid(gb) * ub
    x = x + g2[:, None, :] * (m @ w_o)
    h = modulate(x, norm3, sh3, sc3)
    x = x + 0.5 * g3[:, None, :] * swiglu(h, ffn2_gate, ffn2_up, ffn2_down)
    return x, pool_buf[:, -POOL_STATE:], conv_buf[:, -CONV_STATE:]


def setup_inputs(seed: int = 0) -> dict:
    key = jax.random.key(seed)
    ks = jax.random.split(key, 32)
    f32 = jnp.float32
    def nrm(k, shape, scale):
        return jax.random.normal(k, shape, f32) * scale
    L = DEPTH
    return {
        "x_prompt": nrm(ks[0], (BATCH, SEQ, D_MODEL), 1.0),
        "x_sample": nrm(ks[1], (DEC_BATCH, DEC_SEQ, D_MODEL), 1.0),
        "c_prompt": nrm(ks[2], (BATCH, D_MODEL), 1.0),
        "c_sample": nrm(ks[3], (DEC_BATCH, D_MODEL), 1.0),
        "state_pool": nrm(ks[4], (L, DEC_BATCH, POOL_STATE, POOL_WIDTH), 1.0),
        "state_conv": nrm(ks[5], (L, DEC_BATCH, CONV_STATE, CONV_WIDTH), 1.0),
        "w_ada": nrm(ks[6], (L, D_MODEL, N_MOD * D_MODEL), 0.5 * D_MODEL ** -0.5),
        "b_ada": nrm(ks[7], (L, N_MOD * D_MODEL), 0.02),
        "norm1": 1.0 + nrm(ks[8], (L, D_MODEL), 0.02),
        "ffn1_gate": nrm(ks[9], (L, D_MODEL, D_FF), D_MODEL ** -0.5),
        "ffn1_up": nrm(ks[10], (L, D_MODEL, D_FF), D_MODEL ** -0.5),
        "ffn1_down": nrm(ks[11], (L, D_FF, D_MODEL), D_FF ** -0.5),
        "norm2": 1.0 + nrm(ks[12], (L, D_MODEL), 0.02),
        "w_in": nrm(ks[13], (L, D_MODEL, IN_COLS), D_MODEL ** -0.5),
        "pool_grp": nrm(ks[14], (L, POOL_GROUPS, POOL_GROUP_DIM, POOL_GROUP_DIM), POOL_GROUP_DIM ** -0.5),
        "pool_scale": 1.0 + nrm(ks[15], (L, POOL_WIDTH), 0.02),
        "w_branch_a": nrm(ks[16], (L, POOL_WIDTH, D_MODEL), POOL_WIDTH ** -0.5),
        "conv_w": nrm(ks[17], (L, CONV_K, CONV_WIDTH), CONV_K ** -0.5),
        "conv_b": nrm(ks[18], (L, CONV_WIDTH), 0.02),
        "w_branch_b": nrm(ks[19], (L, CONV_WIDTH, D_MODEL), CONV_WIDTH ** -0.5),
        "w_o": nrm(ks[20], (L, D_MODEL, D_MODEL), D_MODEL ** -0.5),
        "norm3": 1.0 + nrm(ks[21], (L, D_MODEL), 0.02),
        "ffn2_gate": nrm(ks[22], (L, D_MODEL, D_FF), D_MODEL ** -0.5),
        "ffn2_up": nrm(ks[23], (L, D_MODEL, D_FF), D_MODEL ** -0.5),
        "ffn2_down": nrm(ks[24], (L, D_FF, D_MODEL), D_FF ** -0.5),
        "norm_final": 1.0 + nrm(ks[25], (D_MODEL,), 0.02),
    }


def reference(x_prompt, x_sample, c_prompt, c_sample, state_pool, state_conv, w_ada, b_ada, norm1,
              ffn1_gate, ffn1_up, ffn1_down, norm2, w_in, pool_grp, pool_scale, w_branch_a, conv_w, conv_b,
              w_branch_b, w_o, norm3, ffn2_gate, ffn2_up, ffn2_down, norm_final):
    xp, xs = x_prompt, x_sample
    pool_p, conv_p, pool_s, conv_s = [], [], [], []
    for d in range(DEPTH):
        params = (w_ada[d], b_ada[d], norm1[d], ffn1_gate[d], ffn1_up[d], ffn1_down[d], norm2[d], w_in[d],
                  pool_grp[d], pool_scale[d], w_branch_a[d], conv_w[d], conv_b[d], w_branch_b[d], w_o[d],
                  norm3[d], ffn2_gate[d], ffn2_up[d], ffn2_down[d])
        zp_pool = jnp.zeros((xp.shape[0], POOL_STATE, POOL_WIDTH), xp.dtype)
        zp_conv = jnp.zeros((xp.shape[0], CONV_STATE, CONV_WIDTH), xp.dtype)
        xp, np_, nc_ = layer(xp, c_prompt, zp_pool, zp_conv, 0, *params)
        xs, ns_, ncs_ = layer(xs, c_sample, state_pool[d], state_conv[d], PAST_LEN, *params)
        pool_p.append(np_); conv_p.append(nc_); pool_s.append(ns_); conv_s.append(ncs_)
    y_prompt = rmsnorm(xp, norm_final)
    y_sample = rmsnorm(xs, norm_final)
    new_pool_prompt = jnp.stack(pool_p, axis=0)
    new_conv_prompt = jnp.stack(conv_p, axis=0)
    new_pool_sample = jnp.stack(pool_s, axis=0)
    new_conv_sample = jnp.stack(conv_s, axis=0)
    return (y_prompt, y_sample, new_pool_prompt, new_conv_prompt, new_pool_sample, new_conv_sample)
```

```python
import numpy as np
from contextlib import ExitStack
import concourse.bass as bass
import concourse.mybir as mybir
from concourse.bass_utils import run_bass_kernel_spmd

F32 = mybir.dt.float32
F32R = mybir.dt.float32r
AF = mybir.ActivationFunctionType
ALU = mybir.AluOpType
AX = mybir.AxisListType

D = 2048
DFF = 5632
T = 1056
KC = 16
TTW = 352
NTT = 3
MAIN0 = 16
SAMP0 = 1040
NSAMP = 16
PW = 1024
NSLOT = 4
EPS = 1e-6
WINDOWS = (2, 4, 8, 16)
ENGS = ["tensor", "vector", "scalar", "gpsimd", "sync"]


class _Dummy:
    def then_inc(self, *a, **k):
        return self


class _Cap:
    def __init__(self):
        self.calls = []

    def __getattr__(self, name):
        def f(*a, **k):
            self.calls.append((name, a, k))
            return _Dummy()
        return f


class Prog:
    def __init__(self, nc, stack):
        self.nc = nc
        self.stack = stack
        self.ops = {e: [] for e in ENGS}
        self.semh = {}
        self.cnt = {}
        self.alias = {}
        self.acc = {}

    def _sem(self, name):
        if name not in self.semh:
            self.semh[name] = self.stack.enter_context(self.nc.semaphore(name))
            self.cnt[name] = 0

    TRACKED = ("vector", "scalar")

    def _region(self, ap):
        name = ap.tensor.name
        root = self.alias.get(name, name)
        pitch, npart = ap.ap[0]
        off = ap.offset
        p0 = off // pitch if pitch else 0
        c0 = off % pitch if pitch else off
        span = 0
        for step, cnt in ap.ap[1:]:
            span += (cnt - 1) * abs(step)
        return root, p0, p0 + npart, c0, c0 + span + 1

    def _auto_deps(self, fn, tok):
        cap = _Cap()
        fn(cap)
        deps = []
        for name, args, kw in cap.calls:
            aps = [a for a in list(args) + list(kw.values()) if hasattr(a, "tensor") and hasattr(a, "ap")]
            if not aps:
                continue
            if "out" in kw and hasattr(kw["out"], "tensor"):
                out = kw["out"]
                ins = [a for a in aps if a is not out]
            else:
                out, ins = aps[0], aps[1:]
            wr = self._region(out)
            rds = [self._region(a) for a in ins]
            for (root, pl, ph, fl, fh) in rds:
                for e in self.acc.get(root, ()):
                    if e[5] and e[0] < ph and pl < e[1] and e[2] < fh and fl < e[3]:
                        deps.append(e[4])
            root, pl, ph, fl, fh = wr
            lst = self.acc.setdefault(root, [])
            keep = []
            for e in lst:
                if e[0] < ph and pl < e[1] and e[2] < fh and fl < e[3]:
                    deps.append(e[4])
                    if pl <= e[0] and e[1] <= ph and fl <= e[2] and e[3] <= fh:
                        continue
                keep.append(e)
            keep.append((pl, ph, fl, fh, tok, True))
            self.acc[root] = keep[-96:]
            for (root, pl, ph, fl, fh) in rds:
                lst = self.acc.setdefault(root, [])
                lst[:] = [e for e in lst if not ((not e[5]) and e[4][0] == tok[0] and pl <= e[0] and e[1] <= ph
                                                  and fl <= e[2] and e[3] <= fh)]
                lst.append((pl, ph, fl, fh, tok, False))
                if len(lst) > 96:
                    del lst[0:len(lst) - 96]
        return deps

    def op(self, eng, fn, deps=(), sig=True):
        tok = None
        deps = [d for d in deps if d is not None]
        if eng in self.TRACKED:
            sig = True
        if sig:
            self._sem(eng)
            self.cnt[eng] += 1
            tok = (eng, self.cnt[eng])
        if eng in self.TRACKED:
            deps += [d for d in self._auto_deps(fn, tok) if d != tok]
        self.ops[eng].append((fn, deps, tok, 1))
        return tok

    def dma(self, q, out, in_, deps=(), sem="dm"):
        self._sem(sem)
        self.cnt[sem] += 16
        tok = (sem, self.cnt[sem])
        self.ops[q].append((lambda e, o=out, i=in_: e.dma_start(out=o, in_=i),
                            [d for d in deps if d is not None], tok, 16))
        return tok

    def wait(self, eng, deps):
        self.ops[eng].append((None, [d for d in deps if d is not None], None, 0))

    def emit(self, block):
        LOOK = 32
        for eng in ENGS:
            lst = self.ops[eng]
            own = []
            for fn, deps, tok, n in lst:
                own.append(max([v for (nm, v) in deps if nm == eng], default=0))
            mine = []
            c = 0
            for fn, deps, tok, n in lst:
                mine.append(c)
                if tok is not None and tok[0] == eng:
                    c = tok[1]
            hoist = [0] * len(lst)
            if eng in self.TRACKED:
                for j in range(0, len(lst), LOOK // 2):
                    hi = 0
                    for m in range(j, min(j + LOOK, len(lst))):
                        if own[m] <= mine[j] - 6 and own[m] > hi:
                            hi = own[m]
                    hoist[j] = hi

            def body(e, lst=lst, hoist=hoist, eng=eng):
                seen = {}
                for j, (fn, deps, tok, n) in enumerate(lst):
                    dmax = {}
                    if hoist[j]:
                        dmax[eng] = hoist[j]
                    for (nm, v) in deps:
                        dmax[nm] = max(dmax.get(nm, 0), v)
                    for (nm, v) in dmax.items():
                        if seen.get(nm, 0) < v:
                            e.wait_ge(self.semh[nm], v)
                            seen[nm] = v
                    if fn is not None:
                        ins = fn(e)
                        if tok is not None:
                            ins.then_inc(self.semh[tok[0]], n)
            getattr(block, eng)(body)


def build_nc():
    nc = bass.Bass("TRN2", target_bir_lowering=False)

    def din(name, shape):
        return nc.dram_tensor(name, list(shape), F32, kind="ExternalInput").ap()

    def dout(name, shape):
        return nc.dram_tensor(name, list(shape), F32, kind="ExternalOutput").ap()

    xin = din("xin", [T, D])
    cin = din("cin", [18, D])
    spool = din("spool", [NSAMP, 15, PW])
    sconv = din("sconv", [NSAMP, 2, PW])
    posd = din("pos", [128, 16])
    hmaskd = din("hmask", [128, 1])
    identd = din("ident", [128, 128])
    w_ada = din("w_ada", [D, 9 * D])
    b_ada = din("b_ada", [9 * D])
    norm1 = din("norm1", [D])
    norm2 = din("norm2", [D])
    norm3 = din("norm3", [D])
    normf = din("norm_final", [D])
    f1g = din("ffn1_gate", [D, DFF])
    f1u = din("ffn1_up", [D, DFF])
    f1d = din("ffn1_down", [DFF, D])
    f2g = din("ffn2_gate", [D, DFF])
    f2u = din("ffn2_up", [D, DFF])
    f2d = din("ffn2_down", [DFF, D])
    w_in = din("w_in", [D, 8192])
    pool_grp = din("pool_grp", [4, 256, 256])
    pool_scale = din("pool_scale", [PW])
    w_ba = din("w_branch_a", [PW, D])
    conv_w = din("conv_w", [3, PW])
    conv_b = din("conv_b", [PW])
    w_bb = din("w_branch_b", [PW, D])
    w_o = din("w_o", [D, D])

    ymain = dout("ymain", [1024, D])
    ysamp = dout("ysamp", [NSAMP, D])
    poolp = dout("poolp", [15, PW])
    convp = dout("convp", [2, PW])
    pools = dout("pools", [NSAMP, 15, PW])
    convs = dout("convs", [NSAMP, 2, PW])

    with ExitStack() as st:
        def sb(name, shape, dt=F32):
            return st.enter_context(nc.sbuf_tensor(name, list(shape), dt))

        Xt = sb("X", [128, KC * T])
        Hr = sb("H", [128, KC * T], F32R)
        SCRr = sb("ACTS", [128, 4 * T], F32R)
        Ht = nc.alloc_sbuf_tensor_at("Hf", [128, KC * T], F32, offset=nc.lookup_mloc(Hr).addr)
        SCRt = nc.alloc_sbuf_tensor_at("SCRf", [128, 4 * T], F32, offset=nc.lookup_mloc(SCRr).addr)
        RSt = sb("RS", [128, T])
        slots = [sb(f"slot{i}", [128, 2048], F32R) for i in range(NSLOT)]
        slotf = [nc.alloc_sbuf_tensor_at(f"slotf{i}", [128, 2048], F32, offset=nc.lookup_mloc(slots[i]).addr)
                 for i in range(NSLOT)]
        modp = sb("modp", [128, 144])
        A5 = sb("A5", [128, 2048])
        mods2 = sb("mods2", [128, 6 * 16 * NSAMP])
        vecT = sb("vecT", [128, 248])
        ident = sb("identsb", [128, 128])
        onesR = sb("onesR", [128, 128], F32R)
        ct = Ht[0:18, 8192:8192 + D]
        vs0 = Ht[:, 10240:10368]
        vs1 = Ht[:, 10368:10496]
        ones32 = Ht[:, 10496:10624]
        scT = sb("scT", [128, KC * 18], F32R)
        Ut = sb("U", [128, KC * NSAMP])
        ssumT = sb("ssumT", [128, 8 * NSAMP])
        scv = sb("scv", [128, 8 * 32])
        post = sb("post", [128, 16])
        icnt = sb("icnt", [128, 4 * 16])
        hmask = sb("hmasksb", [128, 1])
        ucar = sb("ucar", [128, 8 * 32])
        tmp16 = sb("tmp16", [128, KC * NSAMP])
        pss = [st.enter_context(nc.psum_tensor(f"ps{i}", [128, 512], F32)) for i in range(8)]

        pcar = nc.alloc_sbuf_tensor_at("pcar", [128, 8 * 32], F32, offset=nc.lookup_mloc(scT).addr)

        P = Prog(nc, st)
        for al, root in [(Ht, Hr), (SCRt, SCRr), (pcar, scT)] + [(slotf[i], slots[i]) for i in range(NSLOT)]:
            P.alias[al.name] = root.name

        X3 = Xt[:, :].rearrange("p (k t) -> p k t", k=KC)
        H3 = Ht[:, :].rearrange("p (k t) -> p k t", k=KC)
        H3r = Hr[:, :].rearrange("p (k t) -> p k t", k=KC)
        U3 = Ut[:, :].rearrange("p (k s) -> p k s", k=KC)
        def mods_b(blk):
            if blk < 3:
                v = A5[:, blk * 256:(blk + 1) * 256]
            else:
                v = mods2[:, (blk - 3) * 256:(blk - 2) * 256]
            return v.rearrange("p (m s) -> p m s", s=NSAMP)
        scT3 = scT[:, :].rearrange("p (k c) -> p k c", c=18)
        tmp3 = tmp16[:, :].rearrange("p (k s) -> p k s", k=KC)

        def ttc(tt):
            return slice(tt * TTW, (tt + 1) * TTW)

        bank_readers = [[] for _ in range(8)]

        def bank_deps(b):
            d = bank_readers[b]
            bank_readers[b] = []
            return d

        def mm_group(b, ncols, pairs, deps=(), col0=0, sig=True):
            d0 = list(deps) + bank_deps(b)
            tok = None
            n = len(pairs)
            for idx, (l, r) in enumerate(pairs):
                last = idx == n - 1
                tok = P.op("tensor",
                           lambda e, l=l, r=r, s=(idx == 0), t=last, b=b, c0=col0, nc_=ncols:
                           e.matmul(pss[b][:, c0:c0 + nc_], l, r, start=s, stop=t),
                           deps=d0 if idx == 0 else (), sig=(last and sig))
            return tok

        def transpose(b, col0, in_ap, nrows, ncols, deps=(), first=False, sig=True):
            d0 = list(deps) + (bank_deps(b) if first else [])
            return P.op("tensor",
                        lambda e, b=b, c0=col0, i=in_ap, nr=nrows, ncl=ncols:
                        e.transpose(pss[b][0:ncl, c0:c0 + nr], i, ident[0:nr, 0:nr]),
                        deps=d0, sig=sig)

        ring_order = list(range(NSLOT))
        ring_free = {i: [] for i in range(NSLOT)}

        def ring_next():
            i = ring_order.pop(0)
            ring_order.append(i)
            return i

        def ring_load(src_ap, shape3=None):
            i = ring_next()
            dst = slots[i][:, :]
            if shape3 is not None:
                dst = dst.rearrange("p (a c) -> p a c", a=shape3)
            tok = P.dma("gpsimd", dst, src_ap, deps=list(ring_free[i]), sem=f"w{i}")
            return i, tok

        def ring_release(i, tok):
            ring_free[i] = [tok]

        def ring_claim():
            i = ring_next()
            return i, list(ring_free[i])

        def slot3(i, a):
            return slots[i][:, :].rearrange("p (a c) -> p a c", a=a)

        def wview_kh(W, col0, kh):
            return W[kh * 1024:(kh + 1) * 1024, col0:col0 + 256].rearrange("(k p) c -> p k c", p=128)

        t_id = P.dma("sync", ident[:, :], identd[:, :], sem="c0")
        t_pos = P.dma("sync", post[:, :], posd[:, :], sem="c0")
        t_hm = P.dma("sync", hmask[:, :], hmaskd[:, :], sem="c0")
        t_ct = P.dma("sync", ct[:, :], cin[:, :], sem="c0")
        bview = b_ada.rearrange("(m p) -> m p", p=128)
        P.dma("sync", vs0[:, :], bview[0:128, :], sem="c0")
        P.dma("sync", vs1[0:16, :], bview[128:144, :], sem="c0")
        P.dma("sync", vs1[16:32, :], norm1.rearrange("(k p) -> k p", p=128), sem="c0")
        P.dma("sync", vs1[32:48, :], norm2.rearrange("(k p) -> k p", p=128), sem="c0")
        P.dma("sync", vs1[48:64, :], norm3.rearrange("(k p) -> k p", p=128), sem="c0")
        P.dma("sync", vs1[64:80, :], normf.rearrange("(k p) -> k p", p=128), sem="c0")
        P.dma("sync", vs1[80:88, :], pool_scale.rearrange("(k p) -> k p", p=128), sem="c0")
        P.dma("sync", vs1[88:112, :], conv_w.rearrange("a (k p) -> (a k) p", p=128), sem="c0")
        t_c0 = P.dma("sync", vs1[112:120, :], conv_b.rearrange("(k p) -> k p", p=128), sem="c0")

        t_ones = P.op("vector", lambda e: e.memset(ones32[:, :], 1.0))
        t_onesR = P.op("scalar", lambda e: e.activation(onesR[:, :], ones32[:, :], AF.Copy), deps=[t_ones])
        t_uz = P.op("vector", lambda e: e.memset(Ut[:, :], 0.0))

        tA = transpose(0, 0, vs0[0:128, :], 128, 128, deps=[t_c0], first=True, sig=False)
        tA = transpose(0, 128, vs1[0:120, :], 120, 128, deps=[], sig=True)
        t_vec = P.op("vector", lambda e: e.tensor_copy(vecT[:, :], pss[0][:, 0:248]), deps=[tA])
        bank_readers[0].append(t_vec)

        t_cs = P.op("scalar", lambda e: e.activation(ct[:, :], ct[:, :], AF.Silu), deps=[t_c0])
        tB = None
        for k in range(KC):
            tB = transpose(1, k * 18, ct[0:18, k * 128:(k + 1) * 128], 18, 128, deps=[t_cs],
                           first=(k == 0), sig=(k == KC - 1))
        t_scT = P.op("scalar", lambda e: e.activation(scT[:, :], pss[1][:, 0:288], AF.Copy), deps=[tB])
        bank_readers[1].append(t_scT)

        ti = None
        for wi, w in enumerate(WINDOWS):
            ti = P.op("vector", lambda e, wi=wi, w=w: e.tensor_scalar(
                icnt[:, wi * 16:(wi + 1) * 16], post[:, :], 1.0, float(w), op0=ALU.add, op1=ALU.min),
                deps=[t_c0])
        t_icnt = P.op("vector", lambda e: e.reciprocal(icnt[:, :], icnt[:, :]), deps=[ti])

        def pair_mm(slot_srcs, bank_of, ntiles, width, rhs_of, deps, njj=2, rhs_deps=None):
            toks = [[None] * ntiles for _ in range(njj)]
            ns = len(slot_srcs)
            for si, src in enumerate(slot_srcs):
                i, ltok = ring_load(src, 8)
                first = si == 0
                last = si == ns - 1
                t = None
                for jj in range(njj):
                    for ti in range(ntiles):
                        b = bank_of(jj, ti)
                        for kk in range(8):
                            k = si * 8 + kk
                            d = []
                            if first and kk == 0:
                                d += bank_deps(b) + list(deps)
                            if kk == 0 and jj == 0 and ti == 0:
                                d += [ltok]
                            if rhs_deps is not None and jj == 0 and ti == 0:
                                d += [rhs_deps(k)]
                            end_slot = (jj == njj - 1 and ti == ntiles - 1 and kk == 7)
                            fin = last and kk == 7
                            t = P.op("tensor",
                                     lambda e, b=b, i=i, kk=kk, jj=jj, k=k, ti=ti, s_=(first and kk == 0), f_=fin:
                                     e.matmul(pss[b][:, 0:width], slot3(i, 8)[:, kk, jj * 128:(jj + 1) * 128],
                                              rhs_of(k, ti), start=s_, stop=f_),
                                     deps=d, sig=(end_slot or fin))
                            if fin:
                                toks[jj][ti] = t
                ring_release(i, t)
            return toks

        modtmp = [A5[0:18, 768 + i * 256:1024 + i * 256] for i in range(2)]
        modtmp_free = [None, None]
        modtok = {}
        mod_n = [0]
        mod_pending = []

        def mod_cp(blk, mp):
            m0 = blk * 16 + 2 * mp
            col0 = m0 * 128
            s_ = mod_n[0] % 2
            mod_n[0] += 1
            tg = None
            for kh in range(2):
                i, ltok = ring_load(wview_kh(w_ada, col0, kh), 8)
                for kk in range(8):
                    k = kh * 8 + kk
                    d = []
                    if k == 0:
                        d += bank_deps(6) + [t_scT]
                    if kk == 0:
                        d += [ltok]
                    tg = P.op("tensor", lambda e, i=i, kk=kk, k=k: e.matmul(
                        pss[6][0:18, 0:256], scT3[:, k, :], slot3(i, 8)[:, kk, :], start=(k == 0), stop=(k == 15)),
                        deps=d, sig=(kk == 7))
                ring_release(i, tg)
            tc = P.op("scalar", lambda e, s_=s_: e.activation(modtmp[s_][0:18, :], pss[6][0:18, 0:256], AF.Copy),
                      deps=[tg, modtmp_free[s_]])
            bank_readers[6].append(tc)
            if mod_pending:
                mod_pending.pop(0)()
            mod_pending.append(lambda: mod_cp_b(blk, mp, m0, s_, tc))

        def mod_flush():
            while mod_pending:
                mod_pending.pop(0)()

        def mod_cp_b(blk, mp, m0, s_, tc):
            tq = None
            for jj in range(2):
                tq = transpose(7, jj * 18, modtmp[s_][0:18, jj * 128:(jj + 1) * 128], 18, 128, deps=[tc],
                               first=(jj == 0), sig=(jj == 1))
            modtmp_free[s_] = tq
            psv = pss[7][:, 0:36].rearrange("p (m c) -> p m c", c=18)
            bcol = vecT[:, m0:m0 + 2]
            P.op("vector", lambda e, psv=psv, bcol=bcol, m0=m0:
                 e.tensor_tensor(modp[:, m0:m0 + 2], psv[:, :, 0], bcol, ALU.add), deps=[tq, t_vec], sig=False)
            t2 = P.op("vector", lambda e, psv=psv, bcol=bcol, m0=m0:
                      e.tensor_tensor(mods_b(blk)[:, 2 * mp:2 * mp + 2, :], psv[:, :, 1:17],
                                      bcol.unsqueeze(2).to_broadcast([128, 2, NSAMP]), ALU.add),
                      deps=[tq, t_vec])
            bank_readers[7].append(t2)
            if mp == 7:
                mod_post(blk, t2)

        def mod_post(blk, t2):
            cadd = 1.0 if blk in (1, 4, 7) else 0.0
            cmul = 0.5 if blk in (2, 8) else 1.0
            if cadd != 0.0 or cmul != 1.0:
                P.op("vector", lambda e, blk=blk, ca=cadd, cm=cmul: e.tensor_scalar(
                    modp[:, blk * 16:(blk + 1) * 16], modp[:, blk * 16:(blk + 1) * 16], ca, cm,
                    op0=ALU.add, op1=ALU.mult), deps=[t2], sig=False)
                t2 = P.op("vector", lambda e, blk=blk, ca=cadd, cm=cmul: e.tensor_scalar(
                    mods_b(blk), mods_b(blk), ca, cm,
                    op0=ALU.add, op1=ALU.mult), deps=[t2])
            if blk in (1, 4, 7):
                l = blk // 3
                nv = vecT[:, 144 + 16 * l:160 + 16 * l]
                P.op("vector", lambda e, blk=blk, nv=nv: e.tensor_tensor(
                    modp[:, blk * 16:(blk + 1) * 16], modp[:, blk * 16:(blk + 1) * 16], nv, ALU.mult),
                    deps=[t2], sig=False)
                t2 = P.op("vector", lambda e, blk=blk, nv=nv: e.tensor_tensor(
                    mods_b(blk), mods_b(blk),
                    nv.unsqueeze(2).to_broadcast([128, 16, NSAMP]), ALU.mult), deps=[t2])
            modtok[blk] = t2

        mod_queue = [(blk, mp) for blk in range(2, 9) for mp in range(8)]

        def mod_pop(n):
            for _ in range(n):
                if mod_queue:
                    mod_cp(*mod_queue.pop(0))

        mod_up = [(blk, mp) for blk in range(2) for mp in range(8)]
        stage_free = [None, None]
        xbank = [2, 3]
        nev = 0
        t_xlast = []
        for r in range(9):
            nr = 128 if r < 8 else 32
            s = r % 2
            stg = SCRt[:, s * 2048:(s + 1) * 2048]
            t_ld = P.dma("sync", stg[0:nr, :], xin[r * 128:r * 128 + nr, :], deps=[stage_free[s]], sem=f"xs{s}")
            tl = None
            for kq in range(4):
                b = xbank[(r * 4 + kq) % 2]
                tq = None
                for q in range(4):
                    k = kq * 4 + q
                    tq = transpose(b, q * 128, stg[0:nr, k * 128:(k + 1) * 128], nr, 128, deps=[t_ld],
                                   first=(q == 0), sig=(q == 3))
                src = pss[b][:, 0:512].rearrange("p (a c) -> p a c", a=4)[:, :, 0:nr]
                dst = X3[:, kq * 4:(kq + 1) * 4, r * 128:r * 128 + nr]
                if nev % 2 == 0:
                    te = P.op("vector", lambda e, d=dst, s_=src: e.tensor_copy(d, s_), deps=[tq])
                else:
                    te = P.op("scalar", lambda e, d=dst, s_=src: e.activation(d, s_, AF.Copy), deps=[tq])
                nev += 1
                bank_readers[b].append(te)
                tl = tq
                t_xlast.append(te)
            stage_free[s] = tl
            if r < 8:
                mod_cp(*mod_up.pop(0))
        t_x_ready = t_xlast[-8:]

        P.dma("sync", pools[:, 0:14, :], spool[:, 1:15, :], sem="oc")
        P.dma("sync", convs[:, 0:1, :], sconv[:, 1:2, :], sem="oc")
        stg0 = Ht[0:120, 0:1024]
        stg1 = Ht[0:120, 1024:2048]
        stgc = Ht[0:32, 2048:3072]
        spT = Ht[:, 4096:4096 + 8 * 240].rearrange("p (c q) -> p c q", c=8)
        P.dma("sync", stg0, spool[0:8, :, :].rearrange("s r c -> (s r) c"), sem="sst")
        P.dma("sync", stg1, spool[8:16, :, :].rearrange("s r c -> (s r) c"), sem="sst")
        t_sst = P.dma("sync", stgc, sconv.rearrange("s r c -> (s r) c"), sem="sst")
        t_h_free = []
        for c in range(8):
            b = 4 + (c % 2)
            transpose(b, 0, Ht[0:120, c * 128:(c + 1) * 128], 120, 128, deps=[t_sst], first=True, sig=False)
            transpose(b, 120, Ht[0:120, 1024 + c * 128:1024 + (c + 1) * 128], 120, 128, sig=False)
            tq = transpose(b, 240, Ht[0:32, 2048 + c * 128:2048 + (c + 1) * 128], 32, 128, sig=True)
            te = P.op("scalar", lambda e, c=c, b=b: e.activation(spT[:, c, :], pss[b][:, 0:240], AF.Copy), deps=[tq])
            te2 = P.op("scalar", lambda e, c=c, b=b: e.activation(scv[:, c * 32:(c + 1) * 32], pss[b][:, 240:272], AF.Copy),
                       deps=[tq])
            bank_readers[b].append(te2)
            w = WINDOWS[c // 2]
            src = spT[:, c, :].rearrange("p (s r) -> p s r", r=15)[:, :, 15 - (w - 1):15]
            tr = P.op("vector", lambda e, c=c, src=src: e.tensor_reduce(
                ssumT[:, c * NSAMP:(c + 1) * NSAMP], src, AX.X, ALU.add), deps=[te])
            t_h_free = [tr, tq]


        def mtoks(l):
            return [modtok.get(3 * l), modtok.get(3 * l + 1), modtok.get(3 * l + 2)]

        def Bp(l, k):
            return modp[:, (3 * l) * 16 + k:(3 * l) * 16 + k + 1]

        def Ap(l, k):
            return modp[:, (3 * l + 1) * 16 + k:(3 * l + 1) * 16 + k + 1]

        def Gp(l, k):
            return modp[:, (3 * l + 2) * 16 + k:(3 * l + 2) * 16 + k + 1]

        def Bs(l):
            return mods_b(3 * l)

        def As(l):
            return mods_b(3 * l + 1)

        def Gs(l):
            return mods_b(3 * l + 2)

        sq_free = [None, None]

        def rms_stats(x_deps, scratch_deps, after_k=None, recip=True):
            toks = [None] * NTT
            for k in range(KC):
                s = k % 2
                sq = SCRr[:, s * T:(s + 1) * T]
                tsq = P.op("scalar", lambda e, k=k, sq=sq: e.activation(sq, X3[:, k, :], AF.Square),
                           deps=list(x_deps) + [sq_free[s]] + list(scratch_deps))
                tl = None
                for tt in range(NTT):
                    d = [tsq, t_onesR]
                    if k == 0:
                        d += bank_deps(tt)
                    tl = P.op("tensor", lambda e, tt=tt, sq=sq, k=k: e.matmul(
                        pss[tt][:, 0:TTW], onesR[:, :], sq[:, ttc(tt)], start=(k == 0), stop=(k == KC - 1)),
                        deps=d, sig=True)
                    toks[tt] = tl
                sq_free[s] = tl
                if after_k is not None:
                    after_k(k)
            tr = None
            tas = []
            for tt in range(NTT):
                ta = P.op("scalar", lambda e, tt=tt: e.activation(
                    RSt[:, ttc(tt)], pss[tt][:, 0:TTW], AF.Sqrt, bias=EPS, scale=1.0 / D), deps=[toks[tt]])
                bank_readers[tt].append(ta)
                tas.append(ta)
                if recip:
                    tr = P.op("vector", lambda e, tt=tt: e.reciprocal(RSt[:, ttc(tt)], RSt[:, ttc(tt)]), deps=[ta])
            return tr if recip else tas

        def modulate_full(l, t_rs, h_deps):
            toks = []
            ts = modulate_samples(l, H3r[:, :, SAMP0:T], [t_rs] + mtoks(l) + list(h_deps))
            for k in range(KC):
                t1 = P.op("vector", lambda e, k=k: e.scalar_tensor_tensor(
                    H3[:, k, 0:SAMP0], X3[:, k, 0:SAMP0], Ap(l, k), RSt[:, 0:SAMP0], ALU.mult, ALU.mult),
                    deps=[t_rs] + mtoks(l) + list(h_deps))
                t2 = P.op("scalar", lambda e, k=k: e.activation(
                    H3r[:, k, 0:SAMP0], H3[:, k, 0:SAMP0], AF.Identity, bias=Bp(l, k)), deps=[t1])
                toks.append(t2)
            return toks, [ts]

        def modulate_samples(l, out_ap, deps):
            rsb = RSt[:, SAMP0:T].unsqueeze(1).to_broadcast([128, KC, NSAMP])
            t1 = P.op("vector", lambda e: e.tensor_tensor(tmp3, X3[:, :, SAMP0:T], rsb, ALU.mult), deps=deps)
            t2 = P.op("vector", lambda e: e.tensor_tensor(tmp3, tmp3, As(l), ALU.mult), deps=[t1])
            t3 = P.op("vector", lambda e: e.tensor_tensor(out_ap, tmp3, Bs(l), ALU.add), deps=[t2])
            return t3


        def ffn(l, Wg, Wu, Wd, t_h, interleave_mod=False):
            nd = 0
            t_acc = None
            for g in range(DFF // 512):
                act_tok = [[None] * NTT for _ in range(4)]
                for cpi in range(2):
                    cp = 2 * g + cpi
                    srcs = [wview_kh(Wg, cp * 256, kh) for kh in range(2)]
                    gt = pair_mm(srcs, lambda jj, ti: jj * 3 + ti, NTT, TTW,
                                 lambda k, ti: H3r[:, k, ttc(ti)], t_h[1], rhs_deps=lambda k: t_h[0][k])
                    sg_tok = [[None] * NTT for _ in range(2)]
                    for jj in range(2):
                        ja = 2 * cpi + jj
                        for tt in range(NTT):
                            b = jj * 3 + tt
                            sgv = SCRt[:, ja * T + tt * TTW: ja * T + (tt + 1) * TTW]
                            ts = P.op("scalar", lambda e, b=b, sgv=sgv: e.activation(sgv, pss[b][:, 0:TTW], AF.Silu),
                                      deps=[gt[jj][tt]])
                            bank_readers[b].append(ts)
                            sg_tok[jj][tt] = ts
                    if interleave_mod:
                        mod_pop(2 if g == 0 else 1)
                    srcs = [wview_kh(Wu, cp * 256, kh) for kh in range(2)]
                    ut = pair_mm(srcs, lambda jj, ti: jj * 3 + ti, NTT, TTW,
                                 lambda k, ti: H3r[:, k, ttc(ti)], t_h[1], rhs_deps=lambda k: t_h[0][k])
                    for jj in range(2):
                        ja = 2 * cpi + jj
                        for tt in range(NTT):
                            b = jj * 3 + tt
                            sgv = SCRt[:, ja * T + tt * TTW: ja * T + (tt + 1) * TTW]
                            av = SCRr[:, ja * T + tt * TTW: ja * T + (tt + 1) * TTW]
                            tm = P.op("vector", lambda e, b=b, sgv=sgv, av=av: e.tensor_tensor(
                                av, pss[b][:, 0:TTW], sgv, ALU.mult), deps=[ut[jj][tt], sg_tok[jj][tt]])
                            bank_readers[b].append(tm)
                            act_tok[ja][tt] = tm
                    if interleave_mod:
                        mod_pop(2 if g == 0 else 1)
                if interleave_mod and g == 0:
                    mod_flush()
                for ig in range(4):
                    src = Wd[g * 512:(g + 1) * 512, ig * 512:(ig + 1) * 512].rearrange("(j p) c -> p j c", p=128)
                    si, stok = ring_load(src, 4)
                    tg = None
                    for ii in range(4):
                        i = ig * 4 + ii
                        for tt in range(NTT):
                            b = 6 + (nd % 2)
                            nd += 1
                            pairs = [(slot3(si, 4)[:, ja, ii * 128:(ii + 1) * 128],
                                      SCRr[:, ja * T + tt * TTW: ja * T + (tt + 1) * TTW])
                                     for ja in range(4)]
                            tg = mm_group(b, TTW, pairs, deps=[stok] + [act_tok[ja][tt] for ja in range(4)])
                            if tt < 2:
                                t_acc = P.op("vector", lambda e, b=b, i=i, tt=tt: e.scalar_tensor_tensor(
                                    X3[:, i, ttc(tt)], pss[b][:, 0:TTW], Gp(l, i), X3[:, i, ttc(tt)],
                                    ALU.mult, ALU.add), deps=[tg])
                            else:
                                P.op("vector", lambda e, b=b, i=i: e.scalar_tensor_tensor(
                                    X3[:, i, 2 * TTW:SAMP0], pss[b][:, 0:SAMP0 - 2 * TTW], Gp(l, i),
                                    X3[:, i, 2 * TTW:SAMP0], ALU.mult, ALU.add), deps=[tg], sig=False)
                                t_acc = P.op("vector", lambda e, b=b, i=i: e.tensor_tensor(
                                    U3[:, i, :], pss[b][:, SAMP0 - 2 * TTW:TTW], U3[:, i, :], ALU.add), deps=[tg])
                            bank_readers[b].append(t_acc)
                    ring_release(si, tg)
                    if interleave_mod and ig == 0 and 1 <= g <= 8:
                        mod_pop(1)
            t1 = P.op("vector", lambda e: e.tensor_tensor(tmp3, U3, Gs(l), ALU.mult), deps=[t_acc])
            t2 = P.op("vector", lambda e: e.tensor_tensor(X3[:, :, SAMP0:T], X3[:, :, SAMP0:T], tmp3, ALU.add),
                      deps=[t1])
            t3 = P.op("vector", lambda e: e.memset(Ut[:, :], 0.0), deps=[t2])
            return t3

        t_rs = rms_stats(t_x_ready, [], after_k=lambda k: (mod_cp(*mod_up.pop(0)) if (k % 2 == 1 and mod_up) else None))
        while mod_up:
            mod_cp(*mod_up.pop(0))
        mod_flush()
        t_h = modulate_full(0, t_rs, t_h_free)
        t_x1 = ffn(0, f1g, f1u, f1d, t_h, interleave_mod=True)
        mod_pop(len(mod_queue))
        mod_flush()
        slots.append(nc.alloc_sbuf_tensor_at("slot4", [128, 2048], F32R, offset=nc.lookup_mloc(A5).addr))
        slotf.append(A5)
        P.alias[slots[4].name] = A5.name
        ring_free[4] = [t_x1, modtmp_free[0], modtmp_free[1], modtok[2]]
        ring_order.insert(0, 4)

        t_rs2 = rms_stats([t_x1], [])
        PWD = 528
        TW2 = 264
        HW = KC * PWD
        h2t = Ht[:, 0:HW].rearrange("p (k t) -> p k t", k=KC)
        h2tr = Hr[:, 0:HW].rearrange("p (k t) -> p k t", k=KC)
        avr = Hr[:, HW:2 * HW].rearrange("p (k t) -> p k t", k=KC)
        PB = PWD + 16

        def scr(i, n=PB, off=0):
            return SCRt[:, i * PB + off: i * PB + off + n]

        def scrr(i, n=PWD):
            return SCRr[:, i * PB: i * PB + n]

        def small(i):
            return SCRt[:, 7 * PB + 16 * i: 7 * PB + 16 * (i + 1)]

        def tcols(ti):
            return slice(ti * TW2, (ti + 1) * TW2)

        pcar3 = pcar[:, :].rearrange("p (c s) -> p c s", c=8)
        ucar3 = ucar[:, :].rearrange("p (c s) -> p c s", c=8)
        t_pz = P.op("vector", lambda e: e.memset(pcar[:, :], 0.0), deps=[t_x1])
        t_uz2 = P.op("vector", lambda e: e.memset(ucar[:, :], 0.0), deps=[t_x1])

        pass_deps = [t_rs2, t_pz, t_uz2, t_x1] + mtoks(1)
        nb = [0]

        def nextbank():
            b = nb[0] % 8
            nb[0] += 1
            return b

        out_toks = []
        for ps in range(2):
            c0 = ps * PWD
            hdeps = list(pass_deps)
            th2k = []
            th2 = []
            hw_ = PWD if ps == 0 else PWD - NSAMP
            if ps == 1:
                ts = modulate_samples(1, h2tr[:, :, PWD - NSAMP:PWD], hdeps)
                th2.append(ts)
            for k in range(KC):
                t1 = P.op("vector", lambda e, k=k, c0=c0, hw_=hw_: e.scalar_tensor_tensor(
                    h2t[:, k, 0:hw_], X3[:, k, c0:c0 + hw_], Ap(1, k), RSt[:, c0:c0 + hw_], ALU.mult, ALU.mult),
                    deps=hdeps)
                t2 = P.op("scalar", lambda e, k=k, hw_=hw_: e.activation(
                    h2tr[:, k, 0:hw_], h2t[:, k, 0:hw_], AF.Identity, bias=Bp(1, k)), deps=[t1])
                th2k.append(t2)
            last_dve = [t1]
            last_act = [t2]

            def win_pair(col0):
                bs = [[nextbank(), nextbank()] for _ in range(2)]
                toks = pair_mm([wview_kh(w_in, col0, kh) for kh in range(2)], lambda jj, ti: bs[jj][ti], 2, TW2,
                               lambda k, ti: h2tr[:, k, tcols(ti)], th2, rhs_deps=lambda k: th2k[k])
                return bs, toks

            pb = scr(0)
            apre = scr(3, PWD)
            pb_read = None
            apre_read = None
            apre_toks = []
            for cp in range(4):
                bs, toks = win_pair(cp * 256)
                w = WINDOWS[cp]
                for jj in range(2):
                    c = 2 * cp + jj
                    tcar = P.op("vector", lambda e, c=c: e.tensor_copy(pb[:, 0:16], pcar3[:, c, 16:32]),
                                deps=list(pass_deps))
                    tp = []
                    for ti in range(2):
                        b = bs[jj][ti]
                        t_ = P.op("scalar", lambda e, b=b, ti=ti: e.activation(
                            pb[:, 16 + ti * TW2:16 + (ti + 1) * TW2], pss[b][:, 0:TW2], AF.Copy),
                            deps=[toks[jj][ti], pb_read] + last_dve)
                        bank_readers[b].append(t_)
                        tp.append(t_)
                    tpm = list(tp)
                    if ps == 0:
                        tpm.append(P.op("vector", lambda e: e.tensor_scalar(
                            pb[:, 16:32], pb[:, 16:32], hmask[:, 0:1], None, op0=ALU.mult), deps=[tp[0], t_c0]))
                    tcn = P.op("vector", lambda e, c=c: e.tensor_copy(pcar3[:, c, :], pb[:, PB - 32:PB]),
                               deps=tpm + [tcar])
                    src = pb
                    tprev = tcn
                    nsteps = {2: 1, 4: 2, 8: 3, 16: 4}[w]
                    sh = 1
                    lo = 0
                    for stp in range(nsteps):
                        dstb = scr(1 + (stp % 2))
                        lo2 = lo + sh
                        tprev = P.op("vector", lambda e, src=src, dstb=dstb, lo2=lo2, sh=sh: e.tensor_tensor(
                            dstb[:, lo2:PB], src[:, lo2:PB], src[:, lo2 - sh:PB - sh], ALU.add), deps=[tprev] + tpm)
                        src = dstb
                        lo = lo2
                        sh *= 2
                    ta = P.op("vector", lambda e, src=src, w=w: e.scalar_tensor_tensor(
                        apre, src[:, 16:PB], 1.0 / w, pb[:, 16:PB], ALU.mult, ALU.subtract),
                        deps=[tprev, apre_read])
                    if ps == 0:
                        t5 = P.op("vector", lambda e, src=src, cp=cp: e.tensor_tensor(
                            small(0), src[:, 32:48], icnt[:, cp * 16:(cp + 1) * 16], ALU.mult), deps=[ta, t_icnt])
                        ta = P.op("vector", lambda e: e.tensor_tensor(
                            apre[:, 16:32], small(0), pb[:, 32:48], ALU.subtract), deps=[t5])
                    else:
                        t5 = P.op("vector", lambda e, c=c: e.tensor_tensor(
                            small(0), pb[:, PB - 16:PB], ssumT[:, c * NSAMP:(c + 1) * NSAMP], ALU.add), deps=[ta])
                        ta = P.op("vector", lambda e, w=w: e.scalar_tensor_tensor(
                            apre[:, PWD - 16:PWD], small(0), 1.0 / w, pb[:, PB - 16:PB], ALU.mult, ALU.subtract),
                            deps=[t5])
                    pb_read = ta
                    ts_ = P.op("scalar", lambda e, c=c: e.activation(avr[:, c, :], apre, AF.Copy),
                               deps=[ta] + hdeps)
                    apre_read = ts_
                    apre_toks.append(ts_)
                    last_dve = [ta]
                    last_act = [ts_]
            ccs = [scr(0, PWD), scr(1, PWD)]
            ub = scr(2)
            y1 = scr(3, PWD)
            y2 = scr(4, PWD)
            t_v = []
            ccs_read = [None, None]
            chain_end = []
            ty1_prev = None
            for cp in range(4):
                bsc, tkc = win_pair(2 * PW + cp * 256)
                tcs = [[None, None], [None, None]]
                for jj in range(2):
                    for ti in range(2):
                        b = bsc[jj][ti]
                        tcs[jj][ti] = P.op("scalar", lambda e, b=b, jj=jj, ti=ti: e.activation(
                            ccs[jj][:, tcols(ti)], pss[b][:, 0:TW2], AF.Copy),
                            deps=[tkc[jj][ti], ccs_read[jj]] + last_dve)
                        bank_readers[b].append(tcs[jj][ti])
                bsh, tkh = win_pair(3 * PW + cp * 256)
                bsb, tkb = win_pair(PW + cp * 256)
                for jj in range(2):
                    c = 2 * cp + jj
                    tcar = P.op("vector", lambda e, c=c: e.tensor_copy(ub[:, 0:16], ucar3[:, c, 16:32]),
                                deps=list(pass_deps) + chain_end + [ty1_prev])
                    tu = []
                    for ti in range(2):
                        b = bsh[jj][ti]
                        t_ = P.op("vector", lambda e, b=b, jj=jj, ti=ti: e.tensor_tensor(
                            ub[:, 16 + ti * TW2:16 + (ti + 1) * TW2], pss[b][:, 0:TW2], ccs[jj][:, tcols(ti)], ALU.mult),
                            deps=[tkh[jj][ti], tcs[jj][ti], tcar])
                        bank_readers[b].append(t_)
                        tu.append(t_)
                    ccs_read[jj] = tu[1]
                    if ps == 0:
                        tu.append(P.op("vector", lambda e: e.tensor_scalar(
                            ub[:, 16:32], ub[:, 16:32], hmask[:, 0:1], None, op0=ALU.mult), deps=[tu[0], t_c0]))
                    tcn = P.op("vector", lambda e, c=c: e.tensor_copy(ucar3[:, c, :], ub[:, PB - 32:PB]), deps=tu)
                    cw = lambda kk, c=c: vecT[:, 216 + kk * 8 + c:217 + kk * 8 + c]
                    cbias = vecT[:, 240 + c:241 + c]
                    ty = P.op("scalar", lambda e, cw=cw, cbias=cbias: e.activation(
                        y1, ub[:, 16:PB], AF.Identity, bias=cbias, scale=cw(2)), deps=tu + [tcn] + chain_end)
                    ty1_prev = ty
                    ty = P.op("vector", lambda e, cw=cw: e.scalar_tensor_tensor(
                        y2, ub[:, 15:PB - 1], cw(1), y1, ALU.mult, ALU.add), deps=[ty])
                    ty = P.op("vector", lambda e, cw=cw: e.scalar_tensor_tensor(
                        y1, ub[:, 14:PB - 2], cw(0), y2, ALU.mult, ALU.add), deps=[ty])
                    if ps == 1:
                        sv = scv[:, c * 32:(c + 1) * 32].rearrange("p (s r) -> p s r", r=2)
                        t5 = P.op("scalar", lambda e, cw=cw, cbias=cbias: e.activation(
                            small(1), ub[:, PB - 16:PB], AF.Identity, bias=cbias, scale=cw(2)), deps=[ty])
                        t5 = P.op("vector", lambda e, sv=sv, cw=cw: e.scalar_tensor_tensor(
                            small(2), sv[:, :, 1], cw(1), small(1), ALU.mult, ALU.add), deps=[t5])
                        ty = P.op("vector", lambda e, sv=sv, cw=cw: e.scalar_tensor_tensor(
                            y1[:, PWD - 16:PWD], sv[:, :, 0], cw(0), small(2), ALU.mult, ALU.add), deps=[t5])
                    tv = None
                    for ti in range(2):
                        b = bsb[jj][ti]
                        tv = P.op("vector", lambda e, b=b, c=c, ti=ti: e.tensor_tensor(
                            avr[:, 8 + c, tcols(ti)], pss[b][:, 0:TW2], y1[:, tcols(ti)], ALU.mult),
                            deps=[tkb[jj][ti], ty])
                        bank_readers[b].append(tv)
                    t_v.append(tv)
                    chain_end = [tv]
                    last_dve = [tv]

            si, stok = ring_load(pool_grp.rearrange("g (cc p) d -> p (g cc) d", p=128), 8)
            tg = None
            t_a = []
            for g4 in range(4):
                grp = []
                for dj in range(2):
                    for ti in range(2):
                        b = nextbank()
                        pairs = [(slot3(si, 8)[:, g4 * 2 + cc, dj * 128:(dj + 1) * 128], avr[:, g4 * 2 + cc, tcols(ti)])
                                 for cc in range(2)]
                        tg = mm_group(b, TW2, pairs, deps=[stok] + apre_toks)
                        grp.append((b, dj, ti, tg))
                for (b, dj, ti, tgi) in grp:
                    cidx = g4 * 2 + dj
                    te = P.op("scalar", lambda e, b=b, cidx=cidx, ti=ti: e.activation(
                        avr[:, cidx, tcols(ti)], pss[b][:, 0:TW2], AF.Identity,
                        scale=vecT[:, 208 + cidx:209 + cidx]), deps=[tgi, tg])
                    bank_readers[b].append(te)
                    t_a.append(te)
            ring_release(si, tg)

            sga = [SCRt[:, 0:PWD], SCRt[:, PWD:2 * PWD]]
            sgb = [SCRt[:, 2 * PWD:3 * PWD], SCRt[:, 3 * PWD:4 * PWD]]
            mpr = [SCRr[:, (4 + q) * PWD:(5 + q) * PWD] for q in range(4)]
            tm_quad = []
            tm_prev = []
            two_prev = None
            t_acc = None
            for ip in range(8):
                bsa_, tka_ = win_pair(4 * PW + ip * 256)
                sga_t = [[None, None], [None, None]]
                for jj in range(2):
                    for ti in range(2):
                        b = bsa_[jj][ti]
                        sga_t[jj][ti] = P.op("scalar", lambda e, b=b, jj=jj, ti=ti: e.activation(
                            sga[jj][:, tcols(ti)], pss[b][:, 0:TW2], AF.Sigmoid),
                            deps=[tka_[jj][ti]] + tm_prev + last_dve)
                        bank_readers[b].append(sga_t[jj][ti])
                bsb_, tkb_ = win_pair(4 * PW + D + ip * 256)
                sgb_t = [[None, None], [None, None]]
                for jj in range(2):
                    for ti in range(2):
                        b = bsb_[jj][ti]
                        sgb_t[jj][ti] = P.op("scalar", lambda e, b=b, jj=jj, ti=ti: e.activation(
                            sgb[jj][:, tcols(ti)], pss[b][:, 0:TW2], AF.Sigmoid),
                            deps=[tkb_[jj][ti]] + tm_prev + last_dve)
                        bank_readers[b].append(sgb_t[jj][ti])
                bsu = [[nextbank(), nextbank()] for _ in range(2)]
                tku = pair_mm([w_ba[:, ip * 256:(ip + 1) * 256].rearrange("(k p) c -> p k c", p=128)],
                              lambda jj, ti: bsu[jj][ti], 2, TW2, lambda k, ti: avr[:, k, tcols(ti)], t_a)
                tua = [[None, None], [None, None]]
                for jj in range(2):
                    for ti in range(2):
                        b = bsu[jj][ti]
                        tua[jj][ti] = P.op("vector", lambda e, b=b, jj=jj, ti=ti: e.tensor_tensor(
                            sga[jj][:, tcols(ti)], pss[b][:, 0:TW2], sga[jj][:, tcols(ti)], ALU.mult),
                            deps=[tku[jj][ti], sga_t[jj][ti]])
                        bank_readers[b].append(tua[jj][ti])
                bsv = [[nextbank(), nextbank()] for _ in range(2)]
                tkv = pair_mm([w_bb[:, ip * 256:(ip + 1) * 256].rearrange("(k p) c -> p k c", p=128)],
                              lambda jj, ti: bsv[jj][ti], 2, TW2, lambda k, ti: avr[:, 8 + k, tcols(ti)], t_v[-8:])
                tm = []
                for jj in range(2):
                    tl = None
                    for ti in range(2):
                        b = bsv[jj][ti]
                        tl = P.op("vector", lambda e, b=b, jj=jj, ti=ti: e.tensor_tensor(
                            sgb[jj][:, tcols(ti)], pss[b][:, 0:TW2], sgb[jj][:, tcols(ti)], ALU.mult),
                            deps=[tkv[jj][ti], sgb_t[jj][ti]])
                        bank_readers[b].append(tl)
                    mq = 2 * (ip % 2) + jj
                    tmm = P.op("vector", lambda e, jj=jj, mq=mq: e.tensor_tensor(mpr[mq], sga[jj], sgb[jj], ALU.add),
                               deps=[tl, tua[jj][0], tua[jj][1], two_prev])
                    tm.append(tmm)
                tm_prev = list(tm)
                tm_quad += tm
                last_dve = []
                if ip % 2 == 0:
                    continue
                iq = ip // 2
                tmq = list(tm_quad)
                tm_quad = []
                for half in range(4):
                    src = w_o[iq * 512:(iq + 1) * 512, half * 512:(half + 1) * 512].rearrange(
                        "(kk p) c -> p kk c", p=128)
                    si, stok = ring_load(src, 4)
                    tg = None
                    for oo in range(4):
                        o = half * 4 + oo
                        for ti in range(2):
                            b = nextbank()
                            pairs = [(slot3(si, 4)[:, kk, oo * 128:(oo + 1) * 128], mpr[kk][:, tcols(ti)])
                                     for kk in range(4)]
                            tg = mm_group(b, TW2, pairs, deps=[stok] + tmq)
                            cc0 = c0 + ti * TW2
                            if not (ps == 1 and ti == 1):
                                t_acc = P.op("vector", lambda e, b=b, o=o, cc0=cc0: e.scalar_tensor_tensor(
                                    X3[:, o, cc0:cc0 + TW2], pss[b][:, 0:TW2], Gp(1, o), X3[:, o, cc0:cc0 + TW2],
                                    ALU.mult, ALU.add), deps=[tg])
                            else:
                                wm = TW2 - NSAMP
                                P.op("vector", lambda e, b=b, o=o, cc0=cc0, wm=wm: e.scalar_tensor_tensor(
                                    X3[:, o, cc0:cc0 + wm], pss[b][:, 0:wm], Gp(1, o), X3[:, o, cc0:cc0 + wm],
                                    ALU.mult, ALU.add), deps=[tg], sig=False)
                                t_acc = P.op("vector", lambda e, b=b, o=o, wm=wm: e.tensor_tensor(
                                    U3[:, o, :], pss[b][:, wm:TW2], U3[:, o, :], ALU.add), deps=[tg])
                            bank_readers[b].append(t_acc)
                    ring_release(si, tg)
                    two_prev = tg
            if ps == 1:
                t1 = P.op("vector", lambda e: e.tensor_tensor(tmp3, U3, Gs(1), ALU.mult), deps=[t_acc])
                t2 = P.op("vector", lambda e: e.tensor_tensor(X3[:, :, SAMP0:T], X3[:, :, SAMP0:T], tmp3, ALU.add),
                          deps=[t1])
                t_acc = P.op("vector", lambda e: e.memset(Ut[:, :], 0.0), deps=[t2])
            pass_deps = [t_acc, two_prev]
        t_prev_third = list(pass_deps)

        sti, stfree = ring_claim()
        st_out = slotf[sti][0:32, 0:1024]
        su_out = slotf[sti][0:32, 1024:2048]
        tq = None
        for c in range(8):
            tq = transpose(4 + c // 4, (c % 4) * 128, pcar3[:, c, 1:32], 128, 31, deps=t_prev_third,
                           first=(c % 4 == 0), sig=(c % 4 == 3))
        P.op("vector", lambda e: e.tensor_copy(st_out[0:31, 0:512], pss[4][0:31, 0:512]), deps=[tq] + stfree)
        te = P.op("vector", lambda e: e.tensor_copy(st_out[0:31, 512:1024], pss[5][0:31, 0:512]), deps=[tq] + stfree)
        bank_readers[4].append(te)
        bank_readers[5].append(te)
        out_toks.append(P.dma("sync", poolp[:, :], st_out[0:15, :], deps=[te], sem="oc"))
        out_toks.append(P.dma("sync", pools[:, 14, :], st_out[15:31, :], deps=[te], sem="oc"))
        for c in range(8):
            tq = transpose(6 + c // 4, (c % 4) * 128, ucar3[:, c, 14:32], 128, 18, deps=t_prev_third,
                           first=(c % 4 == 0), sig=(c % 4 == 3))
        P.op("vector", lambda e: e.tensor_copy(su_out[0:18, 0:512], pss[6][0:18, 0:512]), deps=[tq] + stfree)
        te = P.op("vector", lambda e: e.tensor_copy(su_out[0:18, 512:1024], pss[7][0:18, 0:512]), deps=[tq] + stfree)
        bank_readers[6].append(te)
        bank_readers[7].append(te)
        out_toks.append(P.dma("sync", convp[:, :], su_out[0:2, :], deps=[te], sem="oc"))
        tlast = P.dma("sync", convs[:, 1, :], su_out[2:18, :], deps=[te], sem="oc")
        out_toks.append(tlast)
        ring_release(sti, tlast)

        t_rs3 = rms_stats(t_prev_third, [])
        t_h3 = modulate_full(2, t_rs3, t_prev_third)
        t_x3 = ffn(2, f2g, f2u, f2d, t_h3)

        t_sq4 = rms_stats([t_x3], [], recip=False)
        ostage_free = [None, None]
        nev = 0
        for cb in range(9):
            nr = 128 if cb < 8 else NSAMP
            col0 = MAIN0 + cb * 128
            s = cb % 2
            stg = SCRt[:, s * 2048:(s + 1) * 2048]
            tes = []
            trec = P.op("vector", lambda e, col0=col0, nr=nr: e.reciprocal(
                RSt[:, col0:col0 + nr], RSt[:, col0:col0 + nr]), deps=t_sq4)
            ty = None
            for k in range(KC):
                ty = P.op("vector", lambda e, k=k, col0=col0, nr=nr: e.scalar_tensor_tensor(
                    H3[:, k, col0:col0 + nr], X3[:, k, col0:col0 + nr], vecT[:, 192 + k:193 + k],
                    RSt[:, col0:col0 + nr], ALU.mult, ALU.mult), deps=[trec, t_x3])
            for kq in range(4):
                b = (cb * 4 + kq) % 6
                tq = None
                for q in range(4):
                    k = kq * 4 + q
                    tq = transpose(b, q * 128, H3[:, k, col0:col0 + nr], 128, nr, deps=[ty],
                                   first=(q == 0), sig=(q == 3))
                dst = stg[0:nr, kq * 512:(kq + 1) * 512]
                src = pss[b][0:nr, 0:512]
                if nev % 2 == 0:
                    te = P.op("vector", lambda e, d=dst, s_=src: e.tensor_copy(d, s_), deps=[tq, ostage_free[s]])
                else:
                    te = P.op("scalar", lambda e, d=dst, s_=src: e.activation(d, s_, AF.Copy),
                              deps=[tq, ostage_free[s]])
                nev += 1
                bank_readers[b].append(te)
                tes.append(te)
            if cb < 8:
                to = P.dma("sync", ymain[cb * 128:(cb + 1) * 128, :], stg[0:128, :], deps=tes, sem=f"ys{s}")
            else:
                to = P.dma("sync", ysamp[:, :], stg[0:NSAMP, :], deps=tes, sem=f"ys{s}")
            ostage_free[s] = to
            out_toks.append(to)
        P.wait("sync", out_toks + [("oc", P.cnt["oc"])])

        with nc.Block() as block:
            P.emit(block)
    return nc


def mm_group_nobank(P, pss, b, ncols, pairs, deps, col0):
    tok = None
    n = len(pairs)
    for idx, (l, r) in enumerate(pairs):
        last = idx == n - 1
        tok = P.op("tensor",
                   lambda e, l=l, r=r, s=(idx == 0), t=last, b=b, c0=col0, nc_=ncols:
                   e.matmul(pss[b][:, c0:c0 + nc_], l, r, start=s, stop=t),
                   deps=deps if idx == 0 else (), sig=last)
    return tok


_NC_CACHE = {}


def kernel(x_prompt, x_sample, c_prompt, c_sample, state_pool, state_conv, w_ada, b_ada, norm1,
           ffn1_gate, ffn1_up, ffn1_down, norm2, w_in, pool_grp, pool_scale, w_branch_a, conv_w, conv_b,
           w_branch_b, w_o, norm3, ffn2_gate, ffn2_up, ffn2_down, norm_final):
    f = lambda a: np.ascontiguousarray(np.asarray(a, dtype=np.float32))
    x_prompt, x_sample, c_prompt, c_sample = f(x_prompt), f(x_sample), f(c_prompt), f(c_sample)
    state_pool, state_conv = f(state_pool), f(state_conv)
    shared = {
        "w_ada": f(w_ada)[0], "b_ada": f(b_ada)[0], "norm1": f(norm1)[0], "norm2": f(norm2)[0],
        "norm3": f(norm3)[0], "norm_final": f(norm_final),
        "ffn1_gate": f(ffn1_gate)[0], "ffn1_up": f(ffn1_up)[0], "ffn1_down": f(ffn1_down)[0],
        "ffn2_gate": f(ffn2_gate)[0], "ffn2_up": f(ffn2_up)[0], "ffn2_down": f(ffn2_down)[0],
        "w_in": f(w_in)[0], "pool_grp": f(pool_grp)[0], "pool_scale": f(pool_scale)[0],
        "w_branch_a": f(w_branch_a)[0], "conv_w": f(conv_w)[0], "conv_b": f(conv_b)[0],
        "w_branch_b": f(w_branch_b)[0], "w_o": f(w_o)[0],
        "ident": np.eye(128, dtype=np.float32),
    }
    n = 8
    in_maps = []
    for c in range(n):
        b, half = c // 2, c % 2
        start = half * 1024
        halo = x_prompt[b, start - 16:start] if half == 1 else np.zeros((16, D), np.float32)
        xin = np.concatenate([halo, x_prompt[b, start:start + 1024], x_sample[16 * c:16 * c + 16, 0, :]], axis=0)
        cin = np.concatenate([c_prompt[b:b + 1], c_sample[16 * c:16 * c + 16], np.zeros((1, D), np.float32)], axis=0)
        pos = np.broadcast_to((start + np.arange(16)).astype(np.float32)[None, :], (128, 16))
        m = dict(shared)
        m.update({
            "xin": np.ascontiguousarray(xin), "cin": np.ascontiguousarray(cin),
            "spool": np.ascontiguousarray(state_pool[0, 16 * c:16 * c + 16]),
            "sconv": np.ascontiguousarray(state_conv[0, 16 * c:16 * c + 16]),
            "pos": np.ascontiguousarray(pos),
            "hmask": np.full((128, 1), float(half), np.float32),
        })
        in_maps.append(m)
    if "nc" not in _NC_CACHE:
        _NC_CACHE["nc"] = build_nc()
    nc = _NC_CACHE["nc"]
    res = run_bass_kernel_spmd(nc, in_maps, core_ids=list(range(n)))
    R = res.results
    y_prompt = np.zeros((4, 2048, D), np.float32)
    y_sample = np.zeros((128, 1, D), np.float32)
    npp = np.zeros((1, 4, 15, PW), np.float32)
    ncp = np.zeros((1, 4, 2, PW), np.float32)
    nps = np.zeros((1, 128, 15, PW), np.float32)
    ncs = np.zeros((1, 128, 2, PW), np.float32)
    for c in range(n):
        b, half = c // 2, c % 2
        y_prompt[b, half * 1024:(half + 1) * 1024] = R[c]["ymain"]
        y_sample[16 * c:16 * c + 16, 0] = R[c]["ysamp"]
        if half == 1:
            npp[0, b] = R[c]["poolp"]
            ncp[0, b] = R[c]["convp"]
        nps[0, 16 * c:16 * c + 16] = R[c]["pools"]
        ncs[0, 16 * c:16 * c + 16] = R[c]["convs"]
    return (y_prompt, y_sample, npp, ncp, nps, ncs)
```

```python
import numpy as np
from contextlib import ExitStack
import concourse.bass as bass
import concourse.mybir as mybir
from concourse.bass_utils import run_bass_kernel_spmd

F32 = mybir.dt.float32
F32R = mybir.dt.float32r
AF = mybir.ActivationFunctionType
ALU = mybir.AluOpType
AX = mybir.AxisListType

D = 2048
DFF = 5632
T = 1056
KC = 16
TTW = 352
NTT = 3
MAIN0 = 16
SAMP0 = 1040
NSAMP = 16
PW = 1024
NSLOT = 4
EPS = 1e-6
WINDOWS = (2, 4, 8, 16)
ENGS = ["tensor", "vector", "scalar", "gpsimd", "sync"]


class _Dummy:
    def then_inc(self, *a, **k):
        return self


class _Cap:
    def __init__(self):
        self.calls = []

    def __getattr__(self, name):
        def f(*a, **k):
            self.calls.append((name, a, k))
            return _Dummy()
        return f


class Prog:
    def __init__(self, nc, stack):
        self.nc = nc
        self.stack = stack
        self.ops = {e: [] for e in ENGS}
        self.semh = {}
        self.cnt = {}
        self.alias = {}
        self.acc = {}

    def _sem(self, name):
        if name not in self.semh:
            self.semh[name] = self.stack.enter_context(self.nc.semaphore(name))
            self.cnt[name] = 0

    TRACKED = ("vector", "scalar")

    def _region(self, ap):
        name = ap.tensor.name
        root = self.alias.get(name, name)
        pitch, npart = ap.ap[0]
        off = ap.offset
        p0 = off // pitch if pitch else 0
        c0 = off % pitch if pitch else off
        span = 0
        for step, cnt in ap.ap[1:]:
            span += (cnt - 1) * abs(step)
        return root, p0, p0 + npart, c0, c0 + span + 1

    def _auto_deps(self, fn, tok):
        cap = _Cap()
        fn(cap)
        deps = []
        for name, args, kw in cap.calls:
            aps = [a for a in list(args) + list(kw.values()) if hasattr(a, "tensor") and hasattr(a, "ap")]
            if not aps:
                continue
            if "out" in kw and hasattr(kw["out"], "tensor"):
                out = kw["out"]
                ins = [a for a in aps if a is not out]
            else:
                out, ins = aps[0], aps[1:]
            wr = self._region(out)
            rds = [self._region(a) for a in ins]
            for (root, pl, ph, fl, fh) in rds:
                for e in self.acc.get(root, ()):
                    if e[5] and e[0] < ph and pl < e[1] and e[2] < fh and fl < e[3]:
                        deps.append(e[4])
            root, pl, ph, fl, fh = wr
            lst = self.acc.setdefault(root, [])
            keep = []
            for e in lst:
                if e[0] < ph and pl < e[1] and e[2] < fh and fl < e[3]:
                    deps.append(e[4])
                    if pl <= e[0] and e[1] <= ph and fl <= e[2] and e[3] <= fh:
                        continue
                keep.append(e)
            keep.append((pl, ph, fl, fh, tok, True))
            self.acc[root] = keep[-96:]
            for (root, pl, ph, fl, fh) in rds:
                lst = self.acc.setdefault(root, [])
                lst[:] = [e for e in lst if not ((not e[5]) and e[4][0] == tok[0] and pl <= e[0] and e[1] <= ph
                                                  and fl <= e[2] and e[3] <= fh)]
                lst.append((pl, ph, fl, fh, tok, False))
                if len(lst) > 96:
                    del lst[0:len(lst) - 96]
        return deps

    def op(self, eng, fn, deps=(), sig=True):
        tok = None
        deps = [d for d in deps if d is not None]
        if eng in self.TRACKED:
            sig = True
        if sig:
            self._sem(eng)
            self.cnt[eng] += 1
            tok = (eng, self.cnt[eng])
        if eng in self.TRACKED:
            deps += [d for d in self._auto_deps(fn, tok) if d != tok]
        self.ops[eng].append((fn, deps, tok, 1))
        return tok

    def dma(self, q, out, in_, deps=(), sem="dm"):
        self._sem(sem)
        self.cnt[sem] += 16
        tok = (sem, self.cnt[sem])
        self.ops[q].append((lambda e, o=out, i=in_: e.dma_start(out=o, in_=i),
                            [d for d in deps if d is not None], tok, 16))
        return tok

    def wait(self, eng, deps):
        self.ops[eng].append((None, [d for d in deps if d is not None], None, 0))

    def emit(self, block):
        LOOK = 32
        for eng in ENGS:
            lst = self.ops[eng]
            own = []
            for fn, deps, tok, n in lst:
                own.append(max([v for (nm, v) in deps if nm == eng], default=0))
            mine = []
            c = 0
            for fn, deps, tok, n in lst:
                mine.append(c)
                if tok is not None and tok[0] == eng:
                    c = tok[1]
            hoist = [0] * len(lst)
            if eng in self.TRACKED:
                for j in range(0, len(lst), LOOK // 2):
                    hi = 0
                    for m in range(j, min(j + LOOK, len(lst))):
                        if own[m] <= mine[j] - 6 and own[m] > hi:
                            hi = own[m]
                    hoist[j] = hi

            def body(e, lst=lst, hoist=hoist, eng=eng):
                seen = {}
                for j, (fn, deps, tok, n) in enumerate(lst):
                    dmax = {}
                    if hoist[j]:
                        dmax[eng] = hoist[j]
                    for (nm, v) in deps:
                        dmax[nm] = max(dmax.get(nm, 0), v)
                    for (nm, v) in dmax.items():
                        if seen.get(nm, 0) < v:
                            e.wait_ge(self.semh[nm], v)
                            seen[nm] = v
                    if fn is not None:
                        ins = fn(e)
                        if tok is not None:
                            ins.then_inc(self.semh[tok[0]], n)
            getattr(block, eng)(body)


def build_nc():
    nc = bass.Bass("TRN2", target_bir_lowering=False)

    def din(name, shape):
        return nc.dram_tensor(name, list(shape), F32, kind="ExternalInput").ap()

    def dout(name, shape):
        return nc.dram_tensor(name, list(shape), F32, kind="ExternalOutput").ap()

    xin = din("xin", [T, D])
    cin = din("cin", [18, D])
    spool = din("spool", [NSAMP, 15, PW])
    sconv = din("sconv", [NSAMP, 2, PW])
    posd = din("pos", [128, 16])
    hmaskd = din("hmask", [128, 1])
    identd = din("ident", [128, 128])
    w_ada = din("w_ada", [D, 9 * D])
    b_ada = din("b_ada", [9 * D])
    norm1 = din("norm1", [D])
    norm2 = din("norm2", [D])
    norm3 = din("norm3", [D])
    normf = din("norm_final", [D])
    f1g = din("ffn1_gate", [D, DFF])
    f1u = din("ffn1_up", [D, DFF])
    f1d = din("ffn1_down", [DFF, D])
    f2g = din("ffn2_gate", [D, DFF])
    f2u = din("ffn2_up", [D, DFF])
    f2d = din("ffn2_down", [DFF, D])
    w_in = din("w_in", [D, 8192])
    pool_grp = din("pool_grp", [4, 256, 256])
    pool_scale = din("pool_scale", [PW])
    w_ba = din("w_branch_a", [PW, D])
    conv_w = din("conv_w", [3, PW])
    conv_b = din("conv_b", [PW])
    w_bb = din("w_branch_b", [PW, D])
    w_o = din("w_o", [D, D])

    ymain = dout("ymain", [1024, D])
    ysamp = dout("ysamp", [NSAMP, D])
    poolp = dout("poolp", [15, PW])
    convp = dout("convp", [2, PW])
    pools = dout("pools", [NSAMP, 15, PW])
    convs = dout("convs", [NSAMP, 2, PW])

    with ExitStack() as st:
        def sb(name, shape, dt=F32):
            return st.enter_context(nc.sbuf_tensor(name, list(shape), dt))

        Xt = sb("X", [128, KC * T])
        Hr = sb("H", [128, KC * T], F32R)
        SCRr = sb("ACTS", [128, 4 * T], F32R)
        Ht = nc.alloc_sbuf_tensor_at("Hf", [128, KC * T], F32, offset=nc.lookup_mloc(Hr).addr)
        SCRt = nc.alloc_sbuf_tensor_at("SCRf", [128, 4 * T], F32, offset=nc.lookup_mloc(SCRr).addr)
        RSt = sb("RS", [128, T])
        slots = [sb(f"slot{i}", [128, 2048], F32R) for i in range(NSLOT)]
        slotf = [nc.alloc_sbuf_tensor_at(f"slotf{i}", [128, 2048], F32, offset=nc.lookup_mloc(slots[i]).addr)
                 for i in range(NSLOT)]
        modp = sb("modp", [128, 144])
        A5 = sb("A5", [128, 2048])
        mods2 = sb("mods2", [128, 6 * 16 * NSAMP])
        vecT = sb("vecT", [128, 248])
        ident = sb("identsb", [128, 128])
        onesR = sb("onesR", [128, 128], F32R)
        ct = Ht[0:18, 8192:8192 + D]
        vs0 = Ht[:, 10240:10368]
        vs1 = Ht[:, 10368:10496]
        ones32 = Ht[:, 10496:10624]
        scT = sb("scT", [128, KC * 18], F32R)
        Ut = sb("U", [128, KC * NSAMP])
        ssumT = sb("ssumT", [128, 8 * NSAMP])
        scv = sb("scv", [128, 8 * 32])
        post = sb("post", [128, 16])
        icnt = sb("icnt", [128, 4 * 16])
        hmask = sb("hmasksb", [128, 1])
        ucar = sb("ucar", [128, 8 * 32])
        tmp16 = sb("tmp16", [128, KC * NSAMP])
        pss = [st.enter_context(nc.psum_tensor(f"ps{i}", [128, 512], F32)) for i in range(8)]

        pcar = nc.alloc_sbuf_tensor_at("pcar", [128, 8 * 32], F32, offset=nc.lookup_mloc(scT).addr)

        P = Prog(nc, st)
        for al, root in [(Ht, Hr), (SCRt, SCRr), (pcar, scT)] + [(slotf[i], slots[i]) for i in range(NSLOT)]:
            P.alias[al.name] = root.name

        X3 = Xt[:, :].rearrange("p (k t) -> p k t", k=KC)
        H3 = Ht[:, :].rearrange("p (k t) -> p k t", k=KC)
        H3r = Hr[:, :].rearrange("p (k t) -> p k t", k=KC)
        U3 = Ut[:, :].rearrange("p (k s) -> p k s", k=KC)
        def mods_b(blk):
            if blk < 3:
                v = A5[:, blk * 256:(blk + 1) * 256]
            else:
                v = mods2[:, (blk - 3) * 256:(blk - 2) * 256]
            return v.rearrange("p (m s) -> p m s", s=NSAMP)
        scT3 = scT[:, :].rearrange("p (k c) -> p k c", c=18)
        tmp3 = tmp16[:, :].rearrange("p (k s) -> p k s", k=KC)

        def ttc(tt):
            return slice(tt * TTW, (tt + 1) * TTW)

        bank_readers = [[] for _ in range(8)]

        def bank_deps(b):
            d = bank_readers[b]
            bank_readers[b] = []
            return d

        def mm_group(b, ncols, pairs, deps=(), col0=0, sig=True):
            d0 = list(deps) + bank_deps(b)
            tok = None
            n = len(pairs)
            for idx, (l, r) in enumerate(pairs):
                last = idx == n - 1
                tok = P.op("tensor",
                           lambda e, l=l, r=r, s=(idx == 0), t=last, b=b, c0=col0, nc_=ncols:
                           e.matmul(pss[b][:, c0:c0 + nc_], l, r, start=s, stop=t),
                           deps=d0 if idx == 0 else (), sig=(last and sig))
            return tok

        def transpose(b, col0, in_ap, nrows, ncols, deps=(), first=False, sig=True):
            d0 = list(deps) + (bank_deps(b) if first else [])
            return P.op("tensor",
                        lambda e, b=b, c0=col0, i=in_ap, nr=nrows, ncl=ncols:
                        e.transpose(pss[b][0:ncl, c0:c0 + nr], i, ident[0:nr, 0:nr]),
                        deps=d0, sig=sig)

        ring_order = list(range(NSLOT))
        ring_free = {i: [] for i in range(NSLOT)}

        def ring_next():
            i = ring_order.pop(0)
            ring_order.append(i)
            return i

        def ring_load(src_ap, shape3=None):
            i = ring_next()
            dst = slots[i][:, :]
            if shape3 is not None:
                dst = dst.rearrange("p (a c) -> p a c", a=shape3)
            tok = P.dma("gpsimd", dst, src_ap, deps=list(ring_free[i]), sem=f"w{i}")
            return i, tok

        def ring_release(i, tok):
            ring_free[i] = [tok]

        def ring_claim():
            i = ring_next()
            return i, list(ring_free[i])

        def slot3(i, a):
            return slots[i][:, :].rearrange("p (a c) -> p a c", a=a)

        def wview_kh(W, col0, kh):
            return W[kh * 1024:(kh + 1) * 1024, col0:col0 + 256].rearrange("(k p) c -> p k c", p=128)

        t_id = P.dma("sync", ident[:, :], identd[:, :], sem="c0")
        t_pos = P.dma("sync", post[:, :], posd[:, :], sem="c0")
        t_hm = P.dma("sync", hmask[:, :], hmaskd[:, :], sem="c0")
        t_ct = P.dma("sync", ct[:, :], cin[:, :], sem="c0")
        bview = b_ada.rearrange("(m p) -> m p", p=128)
        P.dma("sync", vs0[:, :], bview[0:128, :], sem="c0")
        P.dma("sync", vs1[0:16, :], bview[128:144, :], sem="c0")
        P.dma("sync", vs1[16:32, :], norm1.rearrange("(k p) -> k p", p=128), sem="c0")
        P.dma("sync", vs1[32:48, :], norm2.rearrange("(k p) -> k p", p=128), sem="c0")
        P.dma("sync", vs1[48:64, :], norm3.rearrange("(k p) -> k p", p=128), sem="c0")
        P.dma("sync", vs1[64:80, :], normf.rearrange("(k p) -> k p", p=128), sem="c0")
        P.dma("sync", vs1[80:88, :], pool_scale.rearrange("(k p) -> k p", p=128), sem="c0")
        P.dma("sync", vs1[88:112, :], conv_w.rearrange("a (k p) -> (a k) p", p=128), sem="c0")
        t_c0 = P.dma("sync", vs1[112:120, :], conv_b.rearrange("(k p) -> k p", p=128), sem="c0")

        t_ones = P.op("vector", lambda e: e.memset(ones32[:, :], 1.0))
        t_onesR = P.op("scalar", lambda e: e.activation(onesR[:, :], ones32[:, :], AF.Copy), deps=[t_ones])
        t_uz = P.op("vector", lambda e: e.memset(Ut[:, :], 0.0))

        tA = transpose(0, 0, vs0[0:128, :], 128, 128, deps=[t_c0], first=True, sig=False)
        tA = transpose(0, 128, vs1[0:120, :], 120, 128, deps=[], sig=True)
        t_vec = P.op("vector", lambda e: e.tensor_copy(vecT[:, :], pss[0][:, 0:248]), deps=[tA])
        bank_readers[0].append(t_vec)

        t_cs = P.op("scalar", lambda e: e.activation(ct[:, :], ct[:, :], AF.Silu), deps=[t_c0])
        tB = None
        for k in range(KC):
            tB = transpose(1, k * 18, ct[0:18, k * 128:(k + 1) * 128], 18, 128, deps=[t_cs],
                           first=(k == 0), sig=(k == KC - 1))
        t_scT = P.op("scalar", lambda e: e.activation(scT[:, :], pss[1][:, 0:288], AF.Copy), deps=[tB])
        bank_readers[1].append(t_scT)

        ti = None
        for wi, w in enumerate(WINDOWS):
            ti = P.op("vector", lambda e, wi=wi, w=w: e.tensor_scalar(
                icnt[:, wi * 16:(wi + 1) * 16], post[:, :], 1.0, float(w), op0=ALU.add, op1=ALU.min),
                deps=[t_c0])
        t_icnt = P.op("vector", lambda e: e.reciprocal(icnt[:, :], icnt[:, :]), deps=[ti])

        def pair_mm(slot_srcs, bank_of, ntiles, width, rhs_of, deps, njj=2, rhs_deps=None):
            toks = [[None] * ntiles for _ in range(njj)]
            ns = len(slot_srcs)
            for si, src in enumerate(slot_srcs):
                i, ltok = ring_load(src, 8)
                first = si == 0
                last = si == ns - 1
                t = None
                for jj in range(njj):
                    for ti in range(ntiles):
                        b = bank_of(jj, ti)
                        for kk in range(8):
                            k = si * 8 + kk
                            d = []
                            if first and kk == 0:
                                d += bank_deps(b) + list(deps)
                            if kk == 0 and jj == 0 and ti == 0:
                                d += [ltok]
                            if rhs_deps is not None and jj == 0 and ti == 0:
                                d += [rhs_deps(k)]
                            end_slot = (jj == njj - 1 and ti == ntiles - 1 and kk == 7)
                            fin = last and kk == 7
                            wd = width(ti) if callable(width) else width
                            t = P.op("tensor",
                                     lambda e, b=b, i=i, kk=kk, jj=jj, k=k, ti=ti, s_=(first and kk == 0), f_=fin, wd=wd:
                                     e.matmul(pss[b][:, 0:wd], slot3(i, 8)[:, kk, jj * 128:(jj + 1) * 128],
                                              rhs_of(k, ti), start=s_, stop=f_),
                                     deps=d, sig=(end_slot or fin))
                            if fin:
                                toks[jj][ti] = t
                ring_release(i, t)
            return toks

        modtmp = [A5[0:18, 768 + i * 256:1024 + i * 256] for i in range(2)]
        modtmp_free = [None, None]
        modtok = {}
        mod_n = [0]
        mod_pending = []

        def mod_cp(blk, mp):
            m0 = blk * 16 + 2 * mp
            col0 = m0 * 128
            s_ = mod_n[0] % 2
            mod_n[0] += 1
            tg = None
            for kh in range(2):
                i, ltok = ring_load(wview_kh(w_ada, col0, kh), 8)
                for kk in range(8):
                    k = kh * 8 + kk
                    d = []
                    if k == 0:
                        d += bank_deps(6) + [t_scT]
                    if kk == 0:
                        d += [ltok]
                    tg = P.op("tensor", lambda e, i=i, kk=kk, k=k: e.matmul(
                        pss[6][0:18, 0:256], scT3[:, k, :], slot3(i, 8)[:, kk, :], start=(k == 0), stop=(k == 15)),
                        deps=d, sig=(kk == 7))
                ring_release(i, tg)
            tc = P.op("scalar", lambda e, s_=s_: e.activation(modtmp[s_][0:18, :], pss[6][0:18, 0:256], AF.Copy),
                      deps=[tg, modtmp_free[s_]])
            bank_readers[6].append(tc)
            if mod_pending:
                mod_pending.pop(0)()
            mod_pending.append(lambda: mod_cp_b(blk, mp, m0, s_, tc))

        def mod_flush():
            while mod_pending:
                mod_pending.pop(0)()

        def mod_cp_b(blk, mp, m0, s_, tc):
            tq = None
            for jj in range(2):
                tq = transpose(7, jj * 18, modtmp[s_][0:18, jj * 128:(jj + 1) * 128], 18, 128, deps=[tc],
                               first=(jj == 0), sig=(jj == 1))
            modtmp_free[s_] = tq
            psv = pss[7][:, 0:36].rearrange("p (m c) -> p m c", c=18)
            bcol = vecT[:, m0:m0 + 2]
            P.op("vector", lambda e, psv=psv, bcol=bcol, m0=m0:
                 e.tensor_tensor(modp[:, m0:m0 + 2], psv[:, :, 0], bcol, ALU.add), deps=[tq, t_vec], sig=False)
            t2 = P.op("vector", lambda e, psv=psv, bcol=bcol, m0=m0:
                      e.tensor_tensor(mods_b(blk)[:, 2 * mp:2 * mp + 2, :], psv[:, :, 1:17],
                                      bcol.unsqueeze(2).to_broadcast([128, 2, NSAMP]), ALU.add),
                      deps=[tq, t_vec])
            bank_readers[7].append(t2)
            if mp == 7:
                mod_post(blk, t2)

        def mod_post(blk, t2):
            cadd = 1.0 if blk in (1, 4, 7) else 0.0
            cmul = 0.5 if blk in (2, 8) else 1.0
            if cadd != 0.0 or cmul != 1.0:
                P.op("vector", lambda e, blk=blk, ca=cadd, cm=cmul: e.tensor_scalar(
                    modp[:, blk * 16:(blk + 1) * 16], modp[:, blk * 16:(blk + 1) * 16], ca, cm,
                    op0=ALU.add, op1=ALU.mult), deps=[t2], sig=False)
                t2 = P.op("vector", lambda e, blk=blk, ca=cadd, cm=cmul: e.tensor_scalar(
                    mods_b(blk), mods_b(blk), ca, cm,
                    op0=ALU.add, op1=ALU.mult), deps=[t2])
            if blk in (1, 4, 7):
                l = blk // 3
                nv = vecT[:, 144 + 16 * l:160 + 16 * l]
                P.op("vector", lambda e, blk=blk, nv=nv: e.tensor_tensor(
                    modp[:, blk * 16:(blk + 1) * 16], modp[:, blk * 16:(blk + 1) * 16], nv, ALU.mult),
                    deps=[t2], sig=False)
                t2 = P.op("vector", lambda e, blk=blk, nv=nv: e.tensor_tensor(
                    mods_b(blk), mods_b(blk),
                    nv.unsqueeze(2).to_broadcast([128, 16, NSAMP]), ALU.mult), deps=[t2])
            modtok[blk] = t2

        mod_queue = [(blk, mp) for blk in range(2, 9) for mp in range(8)]

        def mod_pop(n):
            for _ in range(n):
                if mod_queue:
                    mod_cp(*mod_queue.pop(0))

        mod_up = [(blk, mp) for blk in range(2) for mp in range(8)]
        stage_free = [None, None]
        xbank = [2, 3]
        nev = 0
        t_xlast = []
        for r in range(9):
            nr = 128 if r < 8 else 32
            s = r % 2
            stg = SCRt[:, s * 2048:(s + 1) * 2048]
            t_ld = P.dma("sync", stg[0:nr, :], xin[r * 128:r * 128 + nr, :], deps=[stage_free[s]], sem=f"xs{s}")
            tl = None
            for kq in range(4):
                b = xbank[(r * 4 + kq) % 2]
                tq = None
                for q in range(4):
                    k = kq * 4 + q
                    tq = transpose(b, q * 128, stg[0:nr, k * 128:(k + 1) * 128], nr, 128, deps=[t_ld],
                                   first=(q == 0), sig=(q == 3))
                src = pss[b][:, 0:512].rearrange("p (a c) -> p a c", a=4)[:, :, 0:nr]
                dst = X3[:, kq * 4:(kq + 1) * 4, r * 128:r * 128 + nr]
                if nev % 2 == 0:
                    te = P.op("vector", lambda e, d=dst, s_=src: e.tensor_copy(d, s_), deps=[tq])
                else:
                    te = P.op("scalar", lambda e, d=dst, s_=src: e.activation(d, s_, AF.Copy), deps=[tq])
                nev += 1
                bank_readers[b].append(te)
                tl = tq
                t_xlast.append(te)
            stage_free[s] = tl
            if r < 8:
                mod_cp(*mod_up.pop(0))
        t_x_ready = t_xlast[-8:]

        P.dma("sync", pools[:, 0:14, :], spool[:, 1:15, :], sem="oc")
        P.dma("sync", convs[:, 0:1, :], sconv[:, 1:2, :], sem="oc")
        stg0 = Ht[0:120, 0:1024]
        stg1 = Ht[0:120, 1024:2048]
        stgc = Ht[0:32, 2048:3072]
        spT = Ht[:, 4096:4096 + 8 * 240].rearrange("p (c q) -> p c q", c=8)
        P.dma("sync", stg0, spool[0:8, :, :].rearrange("s r c -> (s r) c"), sem="sst")
        P.dma("sync", stg1, spool[8:16, :, :].rearrange("s r c -> (s r) c"), sem="sst")
        t_sst = P.dma("sync", stgc, sconv.rearrange("s r c -> (s r) c"), sem="sst")
        t_h_free = []
        for c in range(8):
            b = 4 + (c % 2)
            transpose(b, 0, Ht[0:120, c * 128:(c + 1) * 128], 120, 128, deps=[t_sst], first=True, sig=False)
            transpose(b, 120, Ht[0:120, 1024 + c * 128:1024 + (c + 1) * 128], 120, 128, sig=False)
            tq = transpose(b, 240, Ht[0:32, 2048 + c * 128:2048 + (c + 1) * 128], 32, 128, sig=True)
            te = P.op("scalar", lambda e, c=c, b=b: e.activation(spT[:, c, :], pss[b][:, 0:240], AF.Copy), deps=[tq])
            te2 = P.op("scalar", lambda e, c=c, b=b: e.activation(scv[:, c * 32:(c + 1) * 32], pss[b][:, 240:272], AF.Copy),
                       deps=[tq])
            bank_readers[b].append(te2)
            w = WINDOWS[c // 2]
            src = spT[:, c, :].rearrange("p (s r) -> p s r", r=15)[:, :, 15 - (w - 1):15]
            tr = P.op("vector", lambda e, c=c, src=src: e.tensor_reduce(
                ssumT[:, c * NSAMP:(c + 1) * NSAMP], src, AX.X, ALU.add), deps=[te])
            t_h_free = [tr, tq]


        def mtoks(l):
            return [modtok.get(3 * l), modtok.get(3 * l + 1), modtok.get(3 * l + 2)]

        def Bp(l, k):
            return modp[:, (3 * l) * 16 + k:(3 * l) * 16 + k + 1]

        def Ap(l, k):
            return modp[:, (3 * l + 1) * 16 + k:(3 * l + 1) * 16 + k + 1]

        def Gp(l, k):
            return modp[:, (3 * l + 2) * 16 + k:(3 * l + 2) * 16 + k + 1]

        def Bs(l):
            return mods_b(3 * l)

        def As(l):
            return mods_b(3 * l + 1)

        def Gs(l):
            return mods_b(3 * l + 2)

        sq_free = [None, None]

        def rms_stats(x_deps, scratch_deps, after_k=None, recip=True):
            toks = [None] * NTT
            for k in range(KC):
                s = k % 2
                sq = SCRr[:, s * T:(s + 1) * T]
                tsq = P.op("scalar", lambda e, k=k, sq=sq: e.activation(sq, X3[:, k, :], AF.Square),
                           deps=list(x_deps) + [sq_free[s]] + list(scratch_deps))
                tl = None
                for tt in range(NTT):
                    d = [tsq, t_onesR]
                    if k == 0:
                        d += bank_deps(tt)
                    tl = P.op("tensor", lambda e, tt=tt, sq=sq, k=k: e.matmul(
                        pss[tt][:, 0:TTW], onesR[:, :], sq[:, ttc(tt)], start=(k == 0), stop=(k == KC - 1)),
                        deps=d, sig=True)
                    toks[tt] = tl
                sq_free[s] = tl
                if after_k is not None:
                    after_k(k)
            tr = None
            tas = []
            for tt in range(NTT):
                ta = P.op("scalar", lambda e, tt=tt: e.activation(
                    RSt[:, ttc(tt)], pss[tt][:, 0:TTW], AF.Sqrt, bias=EPS, scale=1.0 / D), deps=[toks[tt]])
                bank_readers[tt].append(ta)
                tas.append(ta)
                if recip:
                    tr = P.op("vector", lambda e, tt=tt: e.reciprocal(RSt[:, ttc(tt)], RSt[:, ttc(tt)]), deps=[ta])
            return tr if recip else tas

        def modulate_full(l, t_rs, h_deps):
            toks = []
            ts = modulate_samples(l, H3r[:, :, SAMP0:T], [t_rs] + mtoks(l) + list(h_deps))
            for k in range(KC):
                t1 = P.op("vector", lambda e, k=k: e.scalar_tensor_tensor(
                    H3[:, k, 0:SAMP0], X3[:, k, 0:SAMP0], Ap(l, k), RSt[:, 0:SAMP0], ALU.mult, ALU.mult),
                    deps=[t_rs] + mtoks(l) + list(h_deps))
                t2 = P.op("scalar", lambda e, k=k: e.activation(
                    H3r[:, k, 0:SAMP0], H3[:, k, 0:SAMP0], AF.Identity, bias=Bp(l, k)), deps=[t1])
                toks.append(t2)
            return toks, [ts]

        def modulate_samples(l, out_ap, deps):
            rsb = RSt[:, SAMP0:T].unsqueeze(1).to_broadcast([128, KC, NSAMP])
            t1 = P.op("vector", lambda e: e.tensor_tensor(tmp3, X3[:, :, SAMP0:T], rsb, ALU.mult), deps=deps)
            t2 = P.op("vector", lambda e: e.tensor_tensor(tmp3, tmp3, As(l), ALU.mult), deps=[t1])
            t3 = P.op("vector", lambda e: e.tensor_tensor(out_ap, tmp3, Bs(l), ALU.add), deps=[t2])
            return t3


        def ffn(l, Wg, Wu, Wd, t_h, interleave_mod=False, c_lo=0):
            nd = 0
            t_acc = None

            def lo_(tt):
                return max(tt * TTW, c_lo)

            def wid(tt):
                return (tt + 1) * TTW - lo_(tt)

            def cs(tt):
                return slice(lo_(tt), (tt + 1) * TTW)

            for g in range(DFF // 512):
                act_tok = [[None] * NTT for _ in range(4)]
                for cpi in range(2):
                    cp = 2 * g + cpi
                    srcs = [wview_kh(Wg, cp * 256, kh) for kh in range(2)]
                    gt = pair_mm(srcs, lambda jj, ti: jj * 3 + ti, NTT, wid,
                                 lambda k, ti: H3r[:, k, cs(ti)], t_h[1], rhs_deps=lambda k: t_h[0][k])
                    sg_tok = [[None] * NTT for _ in range(2)]
                    for jj in range(2):
                        ja = 2 * cpi + jj
                        for tt in range(NTT):
                            b = jj * 3 + tt
                            sgv = SCRt[:, ja * T + lo_(tt): ja * T + (tt + 1) * TTW]
                            ts = P.op("scalar", lambda e, b=b, sgv=sgv, tt=tt: e.activation(
                                sgv, pss[b][:, 0:wid(tt)], AF.Silu), deps=[gt[jj][tt]])
                            bank_readers[b].append(ts)
                            sg_tok[jj][tt] = ts
                    if interleave_mod:
                        mod_pop(2 if g == 0 else 1)
                    srcs = [wview_kh(Wu, cp * 256, kh) for kh in range(2)]
                    ut = pair_mm(srcs, lambda jj, ti: jj * 3 + ti, NTT, wid,
                                 lambda k, ti: H3r[:, k, cs(ti)], t_h[1], rhs_deps=lambda k: t_h[0][k])
                    for jj in range(2):
                        ja = 2 * cpi + jj
                        for tt in range(NTT):
                            b = jj * 3 + tt
                            sgv = SCRt[:, ja * T + lo_(tt): ja * T + (tt + 1) * TTW]
                            av = SCRr[:, ja * T + lo_(tt): ja * T + (tt + 1) * TTW]
                            tm = P.op("vector", lambda e, b=b, sgv=sgv, av=av, tt=tt: e.tensor_tensor(
                                av, pss[b][:, 0:wid(tt)], sgv, ALU.mult), deps=[ut[jj][tt], sg_tok[jj][tt]])
                            bank_readers[b].append(tm)
                            act_tok[ja][tt] = tm
                    if interleave_mod:
                        mod_pop(2 if g == 0 else 1)
                if interleave_mod and g == 0:
                    mod_flush()
                for ig in range(4):
                    src = Wd[g * 512:(g + 1) * 512, ig * 512:(ig + 1) * 512].rearrange("(j p) c -> p j c", p=128)
                    si, stok = ring_load(src, 4)
                    tg = None
                    for ii in range(4):
                        i = ig * 4 + ii
                        for tt in range(NTT):
                            b = 6 + (nd % 2)
                            nd += 1
                            pairs = [(slot3(si, 4)[:, ja, ii * 128:(ii + 1) * 128],
                                      SCRr[:, ja * T + lo_(tt): ja * T + (tt + 1) * TTW])
                                     for ja in range(4)]
                            tg = mm_group(b, wid(tt), pairs, deps=[stok] + [act_tok[ja][tt] for ja in range(4)])
                            if tt < 2:
                                t_acc = P.op("vector", lambda e, b=b, i=i, tt=tt: e.scalar_tensor_tensor(
                                    X3[:, i, cs(tt)], pss[b][:, 0:wid(tt)], Gp(l, i), X3[:, i, cs(tt)],
                                    ALU.mult, ALU.add), deps=[tg])
                            else:
                                P.op("vector", lambda e, b=b, i=i: e.scalar_tensor_tensor(
                                    X3[:, i, 2 * TTW:SAMP0], pss[b][:, 0:SAMP0 - 2 * TTW], Gp(l, i),
                                    X3[:, i, 2 * TTW:SAMP0], ALU.mult, ALU.add), deps=[tg], sig=False)
                                t_acc = P.op("vector", lambda e, b=b, i=i: e.tensor_tensor(
                                    U3[:, i, :], pss[b][:, SAMP0 - 2 * TTW:TTW], U3[:, i, :], ALU.add), deps=[tg])
                            bank_readers[b].append(t_acc)
                    ring_release(si, tg)
                    if interleave_mod and ig == 0 and 1 <= g <= 8:
                        mod_pop(1)
            t1 = P.op("vector", lambda e: e.tensor_tensor(tmp3, U3, Gs(l), ALU.mult), deps=[t_acc])
            t2 = P.op("vector", lambda e: e.tensor_tensor(X3[:, :, SAMP0:T], X3[:, :, SAMP0:T], tmp3, ALU.add),
                      deps=[t1])
            t3 = P.op("vector", lambda e: e.memset(Ut[:, :], 0.0), deps=[t2])
            return t3

        t_rs = rms_stats(t_x_ready, [], after_k=lambda k: (mod_cp(*mod_up.pop(0)) if (k % 2 == 1 and mod_up) else None))
        while mod_up:
            mod_cp(*mod_up.pop(0))
        mod_flush()
        t_h = modulate_full(0, t_rs, t_h_free)
        t_x1 = ffn(0, f1g, f1u, f1d, t_h, interleave_mod=True)
        mod_pop(len(mod_queue))
        mod_flush()
        slots.append(nc.alloc_sbuf_tensor_at("slot4", [128, 2048], F32R, offset=nc.lookup_mloc(A5).addr))
        slotf.append(A5)
        P.alias[slots[4].name] = A5.name
        ring_free[4] = [t_x1, modtmp_free[0], modtmp_free[1], modtok[2]]
        ring_order.insert(0, 4)

        t_rs2 = rms_stats([t_x1], [])
        PWD = 528
        TW2 = 264
        HW = KC * PWD
        h2t = Ht[:, 0:HW].rearrange("p (k t) -> p k t", k=KC)
        h2tr = Hr[:, 0:HW].rearrange("p (k t) -> p k t", k=KC)
        avr = Hr[:, HW:2 * HW].rearrange("p (k t) -> p k t", k=KC)
        PB = PWD + 16

        def scr(i, n=PB, off=0):
            return SCRt[:, i * PB + off: i * PB + off + n]

        def scrr(i, n=PWD):
            return SCRr[:, i * PB: i * PB + n]

        def small(i):
            return SCRt[:, 7 * PB + 16 * i: 7 * PB + 16 * (i + 1)]

        def tcols(ti):
            return slice(ti * TW2, (ti + 1) * TW2)

        pcar3 = pcar[:, :].rearrange("p (c s) -> p c s", c=8)
        ucar3 = ucar[:, :].rearrange("p (c s) -> p c s", c=8)
        t_pz = P.op("vector", lambda e: e.memset(pcar[:, :], 0.0), deps=[t_x1])
        t_uz2 = P.op("vector", lambda e: e.memset(ucar[:, :], 0.0), deps=[t_x1])

        pass_deps = [t_rs2, t_pz, t_uz2, t_x1] + mtoks(1)
        nb = [0]

        def nextbank():
            b = nb[0] % 8
            nb[0] += 1
            return b

        out_toks = []
        for ps in range(2):
            c0 = ps * PWD
            hdeps = list(pass_deps)
            th2k = []
            th2 = []
            hw_ = PWD if ps == 0 else PWD - NSAMP
            if ps == 1:
                ts = modulate_samples(1, h2tr[:, :, PWD - NSAMP:PWD], hdeps)
                th2.append(ts)
            for k in range(KC):
                t1 = P.op("vector", lambda e, k=k, c0=c0, hw_=hw_: e.scalar_tensor_tensor(
                    h2t[:, k, 0:hw_], X3[:, k, c0:c0 + hw_], Ap(1, k), RSt[:, c0:c0 + hw_], ALU.mult, ALU.mult),
                    deps=hdeps)
                t2 = P.op("scalar", lambda e, k=k, hw_=hw_: e.activation(
                    h2tr[:, k, 0:hw_], h2t[:, k, 0:hw_], AF.Identity, bias=Bp(1, k)), deps=[t1])
                th2k.append(t2)
            last_dve = [t1]
            last_act = [t2]

            def win_pair(col0):
                bs = [[nextbank(), nextbank()] for _ in range(2)]
                toks = pair_mm([wview_kh(w_in, col0, kh) for kh in range(2)], lambda jj, ti: bs[jj][ti], 2, TW2,
                               lambda k, ti: h2tr[:, k, tcols(ti)], th2, rhs_deps=lambda k: th2k[k])
                return bs, toks

            pb = scr(0)
            apre = scr(3, PWD)
            pb_read = None
            apre_read = None
            apre_toks = []
            for cp in range(4):
                bs, toks = win_pair(cp * 256)
                w = WINDOWS[cp]
                for jj in range(2):
                    c = 2 * cp + jj
                    tcar = P.op("vector", lambda e, c=c: e.tensor_copy(pb[:, 0:16], pcar3[:, c, 16:32]),
                                deps=list(pass_deps))
                    tp = []
                    for ti in range(2):
                        b = bs[jj][ti]
                        t_ = P.op("scalar", lambda e, b=b, ti=ti: e.activation(
                            pb[:, 16 + ti * TW2:16 + (ti + 1) * TW2], pss[b][:, 0:TW2], AF.Copy),
                            deps=[toks[jj][ti], pb_read] + last_dve)
                        bank_readers[b].append(t_)
                        tp.append(t_)
                    tpm = list(tp)
                    if ps == 0:
                        tpm.append(P.op("vector", lambda e: e.tensor_scalar(
                            pb[:, 16:32], pb[:, 16:32], hmask[:, 0:1], None, op0=ALU.mult), deps=[tp[0], t_c0]))
                    tcn = P.op("vector", lambda e, c=c: e.tensor_copy(pcar3[:, c, :], pb[:, PB - 32:PB]),
                               deps=tpm + [tcar])
                    src = pb
                    tprev = tcn
                    nsteps = {2: 1, 4: 2, 8: 3, 16: 4}[w]
                    sh = 1
                    lo = 0
                    for stp in range(nsteps):
                        dstb = scr(1 + (stp % 2))
                        lo2 = lo + sh
                        tprev = P.op("vector", lambda e, src=src, dstb=dstb, lo2=lo2, sh=sh: e.tensor_tensor(
                            dstb[:, lo2:PB], src[:, lo2:PB], src[:, lo2 - sh:PB - sh], ALU.add), deps=[tprev] + tpm)
                        src = dstb
                        lo = lo2
                        sh *= 2
                    ta = P.op("vector", lambda e, src=src, w=w: e.scalar_tensor_tensor(
                        apre, src[:, 16:PB], 1.0 / w, pb[:, 16:PB], ALU.mult, ALU.subtract),
                        deps=[tprev, apre_read])
                    if ps == 0:
                        t5 = P.op("vector", lambda e, src=src, cp=cp: e.tensor_tensor(
                            small(0), src[:, 32:48], icnt[:, cp * 16:(cp + 1) * 16], ALU.mult), deps=[ta, t_icnt])
                        ta = P.op("vector", lambda e: e.tensor_tensor(
                            apre[:, 16:32], small(0), pb[:, 32:48], ALU.subtract), deps=[t5])
                    else:
                        t5 = P.op("vector", lambda e, c=c: e.tensor_tensor(
                            small(0), pb[:, PB - 16:PB], ssumT[:, c * NSAMP:(c + 1) * NSAMP], ALU.add), deps=[ta])
                        ta = P.op("vector", lambda e, w=w: e.scalar_tensor_tensor(
                            apre[:, PWD - 16:PWD], small(0), 1.0 / w, pb[:, PB - 16:PB], ALU.mult, ALU.subtract),
                            deps=[t5])
                    pb_read = ta
                    ts_ = P.op("scalar", lambda e, c=c: e.activation(avr[:, c, :], apre, AF.Copy),
                               deps=[ta] + hdeps)
                    apre_read = ts_
                    apre_toks.append(ts_)
                    last_dve = [ta]
                    last_act = [ts_]
            ccs = [scr(0, PWD), scr(1, PWD)]
            ub = scr(2)
            y1 = scr(3, PWD)
            y2 = scr(4, PWD)
            t_v = []
            ccs_read = [None, None]
            chain_end = []
            ty1_prev = None
            for cp in range(4):
                bsc, tkc = win_pair(2 * PW + cp * 256)
                tcs = [[None, None], [None, None]]
                for jj in range(2):
                    for ti in range(2):
                        b = bsc[jj][ti]
                        tcs[jj][ti] = P.op("scalar", lambda e, b=b, jj=jj, ti=ti: e.activation(
                            ccs[jj][:, tcols(ti)], pss[b][:, 0:TW2], AF.Copy),
                            deps=[tkc[jj][ti], ccs_read[jj]] + last_dve)
                        bank_readers[b].append(tcs[jj][ti])
                bsh, tkh = win_pair(3 * PW + cp * 256)
                bsb, tkb = win_pair(PW + cp * 256)
                for jj in range(2):
                    c = 2 * cp + jj
                    tcar = P.op("vector", lambda e, c=c: e.tensor_copy(ub[:, 0:16], ucar3[:, c, 16:32]),
                                deps=list(pass_deps) + chain_end + [ty1_prev])
                    tu = []
                    for ti in range(2):
                        b = bsh[jj][ti]
                        t_ = P.op("vector", lambda e, b=b, jj=jj, ti=ti: e.tensor_tensor(
                            ub[:, 16 + ti * TW2:16 + (ti + 1) * TW2], pss[b][:, 0:TW2], ccs[jj][:, tcols(ti)], ALU.mult),
                            deps=[tkh[jj][ti], tcs[jj][ti], tcar])
                        bank_readers[b].append(t_)
                        tu.append(t_)
                    ccs_read[jj] = tu[1]
                    if ps == 0:
                        tu.append(P.op("vector", lambda e: e.tensor_scalar(
                            ub[:, 16:32], ub[:, 16:32], hmask[:, 0:1], None, op0=ALU.mult), deps=[tu[0], t_c0]))
                    tcn = P.op("vector", lambda e, c=c: e.tensor_copy(ucar3[:, c, :], ub[:, PB - 32:PB]), deps=tu)
                    cw = lambda kk, c=c: vecT[:, 216 + kk * 8 + c:217 + kk * 8 + c]
                    cbias = vecT[:, 240 + c:241 + c]
                    ty = P.op("scalar", lambda e, cw=cw, cbias=cbias: e.activation(
                        y1, ub[:, 16:PB], AF.Identity, bias=cbias, scale=cw(2)), deps=tu + [tcn] + chain_end)
                    ty1_prev = ty
                    ty = P.op("vector", lambda e, cw=cw: e.scalar_tensor_tensor(
                        y2, ub[:, 15:PB - 1], cw(1), y1, ALU.mult, ALU.add), deps=[ty])
                    ty = P.op("vector", lambda e, cw=cw: e.scalar_tensor_tensor(
                        y1, ub[:, 14:PB - 2], cw(0), y2, ALU.mult, ALU.add), deps=[ty])
                    if ps == 1:
                        sv = scv[:, c * 32:(c + 1) * 32].rearrange("p (s r) -> p s r", r=2)
                        t5 = P.op("scalar", lambda e, cw=cw, cbias=cbias: e.activation(
                            small(1), ub[:, PB - 16:PB], AF.Identity, bias=cbias, scale=cw(2)), deps=[ty])
                        t5 = P.op("vector", lambda e, sv=sv, cw=cw: e.scalar_tensor_tensor(
                            small(2), sv[:, :, 1], cw(1), small(1), ALU.mult, ALU.add), deps=[t5])
                        ty = P.op("vector", lambda e, sv=sv, cw=cw: e.scalar_tensor_tensor(
                            y1[:, PWD - 16:PWD], sv[:, :, 0], cw(0), small(2), ALU.mult, ALU.add), deps=[t5])
                    tv = None
                    for ti in range(2):
                        b = bsb[jj][ti]
                        tv = P.op("vector", lambda e, b=b, c=c, ti=ti: e.tensor_tensor(
                            avr[:, 8 + c, tcols(ti)], pss[b][:, 0:TW2], y1[:, tcols(ti)], ALU.mult),
                            deps=[tkb[jj][ti], ty])
                        bank_readers[b].append(tv)
                    t_v.append(tv)
                    chain_end = [tv]
                    last_dve = [tv]

            si, stok = ring_load(pool_grp.rearrange("g (cc p) d -> p (g cc) d", p=128), 8)
            tg = None
            t_a = []
            for g4 in range(4):
                grp = []
                for dj in range(2):
                    for ti in range(2):
                        b = nextbank()
                        pairs = [(slot3(si, 8)[:, g4 * 2 + cc, dj * 128:(dj + 1) * 128], avr[:, g4 * 2 + cc, tcols(ti)])
                                 for cc in range(2)]
                        tg = mm_group(b, TW2, pairs, deps=[stok] + apre_toks)
                        grp.append((b, dj, ti, tg))
                for (b, dj, ti, tgi) in grp:
                    cidx = g4 * 2 + dj
                    te = P.op("scalar", lambda e, b=b, cidx=cidx, ti=ti: e.activation(
                        avr[:, cidx, tcols(ti)], pss[b][:, 0:TW2], AF.Identity,
                        scale=vecT[:, 208 + cidx:209 + cidx]), deps=[tgi, tg])
                    bank_readers[b].append(te)
                    t_a.append(te)
            ring_release(si, tg)

            sga = [SCRt[:, 0:PWD], SCRt[:, PWD:2 * PWD]]
            sgb = [SCRt[:, 2 * PWD:3 * PWD], SCRt[:, 3 * PWD:4 * PWD]]
            mpr = [SCRr[:, (4 + q) * PWD:(5 + q) * PWD] for q in range(4)]
            tm_quad = []
            tm_prev = []
            two_prev = None
            t_acc = None
            for ip in range(8):
                bsa_, tka_ = win_pair(4 * PW + ip * 256)
                sga_t = [[None, None], [None, None]]
                for jj in range(2):
                    for ti in range(2):
                        b = bsa_[jj][ti]
                        sga_t[jj][ti] = P.op("scalar", lambda e, b=b, jj=jj, ti=ti: e.activation(
                            sga[jj][:, tcols(ti)], pss[b][:, 0:TW2], AF.Sigmoid),
                            deps=[tka_[jj][ti]] + tm_prev + last_dve)
                        bank_readers[b].append(sga_t[jj][ti])
                bsb_, tkb_ = win_pair(4 * PW + D + ip * 256)
                sgb_t = [[None, None], [None, None]]
                for jj in range(2):
                    for ti in range(2):
                        b = bsb_[jj][ti]
                        sgb_t[jj][ti] = P.op("scalar", lambda e, b=b, jj=jj, ti=ti: e.activation(
                            sgb[jj][:, tcols(ti)], pss[b][:, 0:TW2], AF.Sigmoid),
                            deps=[tkb_[jj][ti]] + tm_prev + last_dve)
                        bank_readers[b].append(sgb_t[jj][ti])
                bsu = [[nextbank(), nextbank()] for _ in range(2)]
                tku = pair_mm([w_ba[:, ip * 256:(ip + 1) * 256].rearrange("(k p) c -> p k c", p=128)],
                              lambda jj, ti: bsu[jj][ti], 2, TW2, lambda k, ti: avr[:, k, tcols(ti)], t_a)
                tua = [[None, None], [None, None]]
                for jj in range(2):
                    for ti in range(2):
                        b = bsu[jj][ti]
                        tua[jj][ti] = P.op("vector", lambda e, b=b, jj=jj, ti=ti: e.tensor_tensor(
                            sga[jj][:, tcols(ti)], pss[b][:, 0:TW2], sga[jj][:, tcols(ti)], ALU.mult),
                            deps=[tku[jj][ti], sga_t[jj][ti]])
                        bank_readers[b].append(tua[jj][ti])
                bsv = [[nextbank(), nextbank()] for _ in range(2)]
                tkv = pair_mm([w_bb[:, ip * 256:(ip + 1) * 256].rearrange("(k p) c -> p k c", p=128)],
                              lambda jj, ti: bsv[jj][ti], 2, TW2, lambda k, ti: avr[:, 8 + k, tcols(ti)], t_v[-8:])
                tm = []
                for jj in range(2):
                    tl = None
                    for ti in range(2):
                        b = bsv[jj][ti]
                        tl = P.op("vector", lambda e, b=b, jj=jj, ti=ti: e.tensor_tensor(
                            sgb[jj][:, tcols(ti)], pss[b][:, 0:TW2], sgb[jj][:, tcols(ti)], ALU.mult),
                            deps=[tkv[jj][ti], sgb_t[jj][ti]])
                        bank_readers[b].append(tl)
                    mq = 2 * (ip % 2) + jj
                    tmm = P.op("vector", lambda e, jj=jj, mq=mq: e.tensor_tensor(mpr[mq], sga[jj], sgb[jj], ALU.add),
                               deps=[tl, tua[jj][0], tua[jj][1], two_prev])
                    tm.append(tmm)
                tm_prev = list(tm)
                tm_quad += tm
                last_dve = []
                if ip % 2 == 0:
                    continue
                iq = ip // 2
                tmq = list(tm_quad)
                tm_quad = []
                for half in range(4):
                    src = w_o[iq * 512:(iq + 1) * 512, half * 512:(half + 1) * 512].rearrange(
                        "(kk p) c -> p kk c", p=128)
                    si, stok = ring_load(src, 4)
                    tg = None
                    for oo in range(4):
                        o = half * 4 + oo
                        for ti in range(2):
                            b = nextbank()
                            pairs = [(slot3(si, 4)[:, kk, oo * 128:(oo + 1) * 128], mpr[kk][:, tcols(ti)])
                                     for kk in range(4)]
                            tg = mm_group(b, TW2, pairs, deps=[stok] + tmq)
                            cc0 = c0 + ti * TW2
                            if not (ps == 1 and ti == 1):
                                t_acc = P.op("vector", lambda e, b=b, o=o, cc0=cc0: e.scalar_tensor_tensor(
                                    X3[:, o, cc0:cc0 + TW2], pss[b][:, 0:TW2], Gp(1, o), X3[:, o, cc0:cc0 + TW2],
                                    ALU.mult, ALU.add), deps=[tg])
                            else:
                                wm = TW2 - NSAMP
                                P.op("vector", lambda e, b=b, o=o, cc0=cc0, wm=wm: e.scalar_tensor_tensor(
                                    X3[:, o, cc0:cc0 + wm], pss[b][:, 0:wm], Gp(1, o), X3[:, o, cc0:cc0 + wm],
                                    ALU.mult, ALU.add), deps=[tg], sig=False)
                                t_acc = P.op("vector", lambda e, b=b, o=o, wm=wm: e.tensor_tensor(
                                    U3[:, o, :], pss[b][:, wm:TW2], U3[:, o, :], ALU.add), deps=[tg])
                            bank_readers[b].append(t_acc)
                    ring_release(si, tg)
                    two_prev = tg
            if ps == 1:
                t1 = P.op("vector", lambda e: e.tensor_tensor(tmp3, U3, Gs(1), ALU.mult), deps=[t_acc])
                t2 = P.op("vector", lambda e: e.tensor_tensor(X3[:, :, SAMP0:T], X3[:, :, SAMP0:T], tmp3, ALU.add),
                          deps=[t1])
                t_acc = P.op("vector", lambda e: e.memset(Ut[:, :], 0.0), deps=[t2])
            pass_deps = [t_acc, two_prev]
        t_prev_third = list(pass_deps)

        sti, stfree = ring_claim()
        st_out = slotf[sti][0:32, 0:1024]
        su_out = slotf[sti][0:32, 1024:2048]
        tq = None
        for c in range(8):
            tq = transpose(4 + c // 4, (c % 4) * 128, pcar3[:, c, 1:32], 128, 31, deps=t_prev_third,
                           first=(c % 4 == 0), sig=(c % 4 == 3))
        P.op("vector", lambda e: e.tensor_copy(st_out[0:31, 0:512], pss[4][0:31, 0:512]), deps=[tq] + stfree)
        te = P.op("vector", lambda e: e.tensor_copy(st_out[0:31, 512:1024], pss[5][0:31, 0:512]), deps=[tq] + stfree)
        bank_readers[4].append(te)
        bank_readers[5].append(te)
        out_toks.append(P.dma("sync", poolp[:, :], st_out[0:15, :], deps=[te], sem="oc"))
        out_toks.append(P.dma("sync", pools[:, 14, :], st_out[15:31, :], deps=[te], sem="oc"))
        for c in range(8):
            tq = transpose(6 + c // 4, (c % 4) * 128, ucar3[:, c, 14:32], 128, 18, deps=t_prev_third,
                           first=(c % 4 == 0), sig=(c % 4 == 3))
        P.op("vector", lambda e: e.tensor_copy(su_out[0:18, 0:512], pss[6][0:18, 0:512]), deps=[tq] + stfree)
        te = P.op("vector", lambda e: e.tensor_copy(su_out[0:18, 512:1024], pss[7][0:18, 0:512]), deps=[tq] + stfree)
        bank_readers[6].append(te)
        bank_readers[7].append(te)
        out_toks.append(P.dma("sync", convp[:, :], su_out[0:2, :], deps=[te], sem="oc"))
        tlast = P.dma("sync", convs[:, 1, :], su_out[2:18, :], deps=[te], sem="oc")
        out_toks.append(tlast)
        ring_release(sti, tlast)

        t_rs3 = rms_stats(t_prev_third, [])
        t_h3 = modulate_full(2, t_rs3, t_prev_third)
        t_x3 = ffn(2, f2g, f2u, f2d, t_h3, c_lo=MAIN0)

        t_sq4 = rms_stats([t_x3], [], recip=False)
        ostage_free = [None, None]
        nev = 0
        for cb in range(9):
            nr = 128 if cb < 8 else NSAMP
            col0 = MAIN0 + cb * 128
            s = cb % 2
            stg = SCRt[:, s * 2048:(s + 1) * 2048]
            tes = []
            trec = P.op("vector", lambda e, col0=col0, nr=nr: e.reciprocal(
                RSt[:, col0:col0 + nr], RSt[:, col0:col0 + nr]), deps=t_sq4)
            ty = None
            for k in range(KC):
                ty = P.op("vector", lambda e, k=k, col0=col0, nr=nr: e.scalar_tensor_tensor(
                    H3[:, k, col0:col0 + nr], X3[:, k, col0:col0 + nr], vecT[:, 192 + k:193 + k],
                    RSt[:, col0:col0 + nr], ALU.mult, ALU.mult), deps=[trec, t_x3])
            for kq in range(4):
                b = (cb * 4 + kq) % 6
                tq = None
                for q in range(4):
                    k = kq * 4 + q
                    tq = transpose(b, q * 128, H3[:, k, col0:col0 + nr], 128, nr, deps=[ty],
                                   first=(q == 0), sig=(q == 3))
                dst = stg[0:nr, kq * 512:(kq + 1) * 512]
                src = pss[b][0:nr, 0:512]
                if nev % 2 == 0:
                    te = P.op("vector", lambda e, d=dst, s_=src: e.tensor_copy(d, s_), deps=[tq, ostage_free[s]])
                else:
                    te = P.op("scalar", lambda e, d=dst, s_=src: e.activation(d, s_, AF.Copy),
                              deps=[tq, ostage_free[s]])
                nev += 1
                bank_readers[b].append(te)
                tes.append(te)
            if cb < 8:
                to = P.dma("sync", ymain[cb * 128:(cb + 1) * 128, :], stg[0:128, :], deps=tes, sem=f"ys{s}")
            else:
                to = P.dma("sync", ysamp[:, :], stg[0:NSAMP, :], deps=tes, sem=f"ys{s}")
            ostage_free[s] = to
            out_toks.append(to)
        P.wait("sync", out_toks + [("oc", P.cnt["oc"])])

        with nc.Block() as block:
            P.emit(block)
    return nc


def mm_group_nobank(P, pss, b, ncols, pairs, deps, col0):
    tok = None
    n = len(pairs)
    for idx, (l, r) in enumerate(pairs):
        last = idx == n - 1
        tok = P.op("tensor",
                   lambda e, l=l, r=r, s=(idx == 0), t=last, b=b, c0=col0, nc_=ncols:
                   e.matmul(pss[b][:, c0:c0 + nc_], l, r, start=s, stop=t),
                   deps=deps if idx == 0 else (), sig=last)
    return tok


_NC_CACHE = {}


def kernel(x_prompt, x_sample, c_prompt, c_sample, state_pool, state_conv, w_ada, b_ada, norm1,
           ffn1_gate, ffn1_up, ffn1_down, norm2, w_in, pool_grp, pool_scale, w_branch_a, conv_w, conv_b,
           w_branch_b, w_o, norm3, ffn2_gate, ffn2_up, ffn2_down, norm_final):
    f = lambda a: np.ascontiguousarray(np.asarray(a, dtype=np.float32))
    x_prompt, x_sample, c_prompt, c_sample = f(x_prompt), f(x_sample), f(c_prompt), f(c_sample)
    state_pool, state_conv = f(state_pool), f(state_conv)
    shared = {
        "w_ada": f(w_ada)[0], "b_ada": f(b_ada)[0], "norm1": f(norm1)[0], "norm2": f(norm2)[0],
        "norm3": f(norm3)[0], "norm_final": f(norm_final),
        "ffn1_gate": f(ffn1_gate)[0], "ffn1_up": f(ffn1_up)[0], "ffn1_down": f(ffn1_down)[0],
        "ffn2_gate": f(ffn2_gate)[0], "ffn2_up": f(ffn2_up)[0], "ffn2_down": f(ffn2_down)[0],
        "w_in": f(w_in)[0], "pool_grp": f(pool_grp)[0], "pool_scale": f(pool_scale)[0],
        "w_branch_a": f(w_branch_a)[0], "conv_w": f(conv_w)[0], "conv_b": f(conv_b)[0],
        "w_branch_b": f(w_branch_b)[0], "w_o": f(w_o)[0],
        "ident": np.eye(128, dtype=np.float32),
    }
    n = 8
    in_maps = []
    for c in range(n):
        b, half = c // 2, c % 2
        start = half * 1024
        halo = x_prompt[b, start - 16:start] if half == 1 else np.zeros((16, D), np.float32)
        xin = np.concatenate([halo, x_prompt[b, start:start + 1024], x_sample[16 * c:16 * c + 16, 0, :]], axis=0)
        cin = np.concatenate([c_prompt[b:b + 1], c_sample[16 * c:16 * c + 16], np.zeros((1, D), np.float32)], axis=0)
        pos = np.broadcast_to((start + np.arange(16)).astype(np.float32)[None, :], (128, 16))
        m = dict(shared)
        m.update({
            "xin": np.ascontiguousarray(xin), "cin": np.ascontiguousarray(cin),
            "spool": np.ascontiguousarray(state_pool[0, 16 * c:16 * c + 16]),
            "sconv": np.ascontiguousarray(state_conv[0, 16 * c:16 * c + 16]),
            "pos": np.ascontiguousarray(pos),
            "hmask": np.full((128, 1), float(half), np.float32),
        })
        in_maps.append(m)
    if "nc" not in _NC_CACHE:
        _NC_CACHE["nc"] = build_nc()
    nc = _NC_CACHE["nc"]
    res = run_bass_kernel_spmd(nc, in_maps, core_ids=list(range(n)))
    R = res.results
    y_prompt = np.zeros((4, 2048, D), np.float32)
    y_sample = np.zeros((128, 1, D), np.float32)
    npp = np.zeros((1, 4, 15, PW), np.float32)
    ncp = np.zeros((1, 4, 2, PW), np.float32)
    nps = np.zeros((1, 128, 15, PW), np.float32)
    ncs = np.zeros((1, 128, 2, PW), np.float32)
    for c in range(n):
        b, half = c // 2, c % 2
        y_prompt[b, half * 1024:(half + 1) * 1024] = R[c]["ymain"]
        y_sample[16 * c:16 * c + 16, 0] = R[c]["ysamp"]
        if half == 1:
            npp[0, b] = R[c]["poolp"]
            ncp[0, b] = R[c]["convp"]
        nps[0, 16 * c:16 * c + 16] = R[c]["pools"]
        ncs[0, 16 * c:16 * c + 16] = R[c]["convs"]
    return (y_prompt, y_sample, npp, ncp, nps, ncs)
```
